# Optimizing a Trainium2 kernel written in Bass

```python
import math
import jax, jax.numpy as jnp
from jax import lax
import numpy as np

D_MODEL = 2048
BATCH = 4
SEQ = 4096
DEPTH = 2
DEC_BATCH = 8
DEC_SEQ = 32
PAST_LEN = 2048

CHUNK = 64
MIX_WIDTH = D_MODEL
EPS = 1e-6
F32 = jnp.float32
A_GROUPS = 4
A_HEAD = 128
A_WIDTH = A_GROUPS * A_HEAD
A_CHUNK = 128
B_HEADS = 4
B_DK = 128
B_DV = 128
B_WIDTH = B_HEADS * B_DV
B_QKV = 2 * B_HEADS * B_DK + B_WIDTH
B_CONV = 4
C_HEADS = 8
C_NOPE = 128
C_ROPE = 64
C_VDIM = 128
C_WIDTH = C_HEADS * C_VDIM
Q_LORA = 512
KV_LORA = 256
ROPE_THETA = 10000.0
C_SCALE = (C_NOPE + C_ROPE) ** -0.5
Q_BLOCK = 128
D_FF = 4 * D_MODEL
OFF_AU = 0
OFF_AV = OFF_AU + A_WIDTH
OFF_BQKV = OFF_AV + A_WIDTH
OFF_BZ = OFF_BQKV + B_QKV
OFF_BBETA = OFF_BZ + B_WIDTH
OFF_BALPHA = OFF_BBETA + B_HEADS
OFF_CQ = OFF_BALPHA + B_HEADS
OFF_CKV = OFF_CQ + Q_LORA
OFF_CKR = OFF_CKV + KV_LORA
IN_COLS = OFF_CKR + C_ROPE

kernel_name = "hymba_gmlp_gdn_mla_stream_step"


def rms_norm(x, w):
    xf = x.astype(F32)
    y = xf * lax.rsqrt(jnp.mean(xf * xf, axis=-1, keepdims=True) + EPS)
    return (y * w.astype(F32)).astype(x.dtype)


def layer_norm(x, w, b):
    xf = x.astype(F32)
    mu = jnp.mean(xf, axis=-1, keepdims=True)
    var = jnp.mean(jnp.square(xf - mu), axis=-1, keepdims=True)
    return ((xf - mu) * lax.rsqrt(var + 1e-5) * w.astype(F32) + b.astype(F32)).astype(x.dtype)


def l2norm(x):
    xf = x.astype(F32)
    return xf * lax.rsqrt(jnp.sum(xf * xf, axis=-1, keepdims=True) + EPS)


def rope(x, pos):
    half = C_ROPE // 2
    inv = ROPE_THETA ** (-jnp.arange(half, dtype=F32) / half)
    ang = pos.astype(F32)[:, None] * inv[None, :]
    shape = (1, pos.shape[0]) + (1,) * (x.ndim - 3) + (half,)
    cos = jnp.cos(ang).reshape(shape).astype(x.dtype)
    sin = jnp.sin(ang).reshape(shape).astype(x.dtype)
    x1, x2 = x[..., :half], x[..., half:]
    return jnp.concatenate([x1 * cos - x2 * sin, x1 * sin + x2 * cos], axis=-1)


def chunk_mlp_mix(u, v, ln_w, ln_b, w_s, b_s):
    bsz, L, _ = u.shape
    lc = min(L, A_CHUNK)
    n = L // lc
    vn = layer_norm(v, ln_w, ln_b)
    idx = jnp.arange(lc)
    allowed = (idx[:, None] // CHUNK) >= (idx[None, :] // CHUNK)
    w = jnp.where(allowed[None], w_s[:, :lc, :lc], 0.0)
    f = jnp.einsum('gij,bnjgc->bnigc', w, vn.reshape(bsz, n, lc, A_GROUPS, A_HEAD))
    f = f + b_s[:, :lc].T[None, None, :, :, None]
    return u * f.reshape(bsz, L, A_WIDTH), vn


def causal_conv(x, buf, w):
    L = x.shape[1]
    xp = jnp.concatenate([buf, x], axis=1)
    y = xp[:, 0:L] * w[0]
    for i in range(1, B_CONV):
        y = y + xp[:, i:i + L] * w[i]
    return jax.nn.silu(y), xp[:, -(B_CONV - 1):]


def gated_delta_rule(q, k, v, g, beta, s0, chunk):
    bsz, L, H, dk = q.shape
    dv = v.shape[-1]
    n = L // chunk

    def to_chunks(t):
        return t.astype(F32).reshape(bsz, n, chunk, H, -1).transpose(0, 3, 1, 2, 4)

    qc = to_chunks(q) * (dk ** -0.5)
    kc = to_chunks(k)
    vc = to_chunks(v)
    gc = jnp.cumsum(g.astype(F32).reshape(bsz, n, chunk, H).transpose(0, 3, 1, 2), axis=-1)
    bc = beta.astype(F32).reshape(bsz, n, chunk, H).transpose(0, 3, 1, 2)[..., None]
    idx = jnp.arange(chunk)
    causal = idx[:, None] >= idx[None, :]
    strict = idx[:, None] > idx[None, :]
    decay = jnp.exp(jnp.where(causal, gc[..., :, None] - gc[..., None, :], -jnp.inf))
    kb = kc * bc
    lower = jnp.where(strict, jnp.einsum('bhnid,bhnjd->bhnij', kb, kc) * decay, 0.0)
    a = lower + jnp.eye(chunk, dtype=F32)
    rhs = jnp.concatenate([vc * bc, kb * jnp.exp(gc)[..., None]], axis=-1)
    sol = lax.linalg.triangular_solve(a, rhs, left_side=True, lower=True, unit_diagonal=True)
    u, w = sol[..., :dv], sol[..., dv:]
    qk = jnp.where(causal, jnp.einsum('bhnid,bhnjd->bhnij', qc, kc) * decay, 0.0)

    def step(S, xs):
        q_i, k_i, u_i, w_i, qk_i, g_i = xs
        v_new = u_i - jnp.einsum('bhck,bhkv->bhcv', w_i, S)
        o_i = (jnp.einsum('bhck,bhkv->bhcv', q_i * jnp.exp(g_i)[..., None], S)
               + jnp.einsum('bhij,bhjv->bhiv', qk_i, v_new))
        g_last = g_i[..., -1:]
        S = S * jnp.exp(g_last)[..., None] + jnp.einsum(
            'bhck,bhcv->bhkv', k_i * jnp.exp(g_last - g_i)[..., None], v_new)
        return S, o_i

    xs = tuple(jnp.moveaxis(t, 2, 0) for t in (qc, kc, u, w, qk, gc))
    s_fin, o = lax.scan(step, s0.astype(F32), xs)
    o = o.transpose(1, 0, 3, 2, 4).reshape(bsz, L, H, dv)
    return o, s_fin


def gated_deltanet(proj, conv_buf, s0, conv_w, a_log, dt_bias, norm_w, chunk):
    bsz, L, _ = proj.shape
    qkv, new_buf = causal_conv(proj[..., OFF_BQKV:OFF_BZ], conv_buf, conv_w)
    hk = B_HEADS * B_DK
    q = l2norm(qkv[..., :hk].reshape(bsz, L, B_HEADS, B_DK))
    k = l2norm(qkv[..., hk:2 * hk].reshape(bsz, L, B_HEADS, B_DK))
    v = qkv[..., 2 * hk:].reshape(bsz, L, B_HEADS, B_DV)
    z = proj[..., OFF_BZ:OFF_BBETA].reshape(bsz, L, B_HEADS, B_DV)
    beta = jax.nn.sigmoid(proj[..., OFF_BBETA:OFF_BALPHA].astype(F32))
    g = -jnp.exp(a_log.astype(F32)) * jax.nn.softplus(proj[..., OFF_BALPHA:OFF_CQ].astype(F32) + dt_bias.astype(F32))
    o, s_new = gated_delta_rule(q, k, v, g, beta, s0, chunk)
    o = o * lax.rsqrt(jnp.mean(o * o, axis=-1, keepdims=True) + EPS) * norm_w.astype(F32) * jax.nn.silu(z.astype(F32))
    return o.astype(proj.dtype).reshape(bsz, L, B_WIDTH), s_new.astype(s0.dtype), new_buf


def mla_project(proj, pos, q_norm_w, w_uq, kv_norm_w, w_uk):
    cq = rms_norm(proj[..., OFF_CQ:OFF_CKV], q_norm_w)
    q = jnp.einsum('bsc,chd->bshd', cq, w_uq)
    q_lat = jnp.einsum('bshd,chd->bshc', q[..., :C_NOPE], w_uk)
    q_rope = rope(q[..., C_NOPE:], pos)
    ckv = rms_norm(proj[..., OFF_CKV:OFF_CKR], kv_norm_w)
    kr = rope(proj[..., OFF_CKR:IN_COLS], pos)
    return q_lat, q_rope, ckv, kr


def mla_attend(q_lat, q_rope, ckv, kr, w_uv, mask):
    s = (jnp.einsum('bqhc,bkc->bhqk', q_lat, ckv).astype(F32)
         + jnp.einsum('bqhr,bkr->bhqk', q_rope, kr).astype(F32)) * C_SCALE
    if mask is not None:
        s = jnp.where(mask, s, -jnp.inf)
    p = jax.nn.softmax(s, axis=-1).astype(ckv.dtype)
    o = jnp.einsum('bhqk,bkc->bqhc', p, ckv)
    return jnp.einsum('bqhc,chv->bqhv', o, w_uv)


def mla_prompt(q_lat, q_rope, ckv, kr, w_uv, pos):
    bsz, L = q_lat.shape[0], q_lat.shape[1]
    nb = L // Q_BLOCK

    def block(args):
        ql, qr, qp = args
        mask = (qp[:, None] // CHUNK) >= (pos[None, :] // CHUNK)
        return mla_attend(ql, qr, ckv, kr, w_uv, mask)

    ql = q_lat.reshape(bsz, nb, Q_BLOCK, C_HEADS, KV_LORA).swapaxes(0, 1)
    qr = q_rope.reshape(bsz, nb, Q_BLOCK, C_HEADS, C_ROPE).swapaxes(0, 1)
    out = lax.map(block, (ql, qr, pos.reshape(nb, Q_BLOCK)))
    return out.swapaxes(0, 1).reshape(bsz, L, C_WIDTH)


def layer(x, pos, lw, past):
    bsz, L, _ = x.shape
    h = rms_norm(x, lw['attn_norm_w'])
    proj = jnp.einsum('bld,de->ble', h, lw['w_in'])
    u = jax.nn.gelu(proj[..., OFF_AU:OFF_AV], approximate=False)
    v = jax.nn.gelu(proj[..., OFF_AV:OFF_BQKV], approximate=False)
    ya, va = chunk_mlp_mix(u, v, lw['a_ln_w'], lw['a_ln_b'], lw['a_w_s'], lw['a_b_s'])
    if past is None:
        conv_buf = jnp.zeros((bsz, B_CONV - 1, B_QKV), x.dtype)
        s0 = jnp.zeros((bsz, B_HEADS, B_DK, B_DV), x.dtype)
        chunk = CHUNK
    else:
        ckv_past, kr_past, s0, conv_buf = past
        chunk = L
    yb, s_new, buf_new = gated_deltanet(proj, conv_buf, s0, lw['b_conv_w'], lw['b_a_log'],
                                        lw['b_dt_bias'], lw['b_norm_w'], chunk)
    q_lat, q_rope, ckv, kr = mla_project(proj, pos, lw['c_q_norm_w'], lw['c_w_uq'], lw['c_kv_norm_w'], lw['c_w_uk'])
    if past is None:
        yc = mla_prompt(q_lat, q_rope, ckv, kr, lw['c_w_uv'], pos)
    else:
        keys_c = jnp.concatenate([ckv_past, ckv], axis=1)
        keys_r = jnp.concatenate([kr_past, kr], axis=1)
        yc = mla_attend(q_lat, q_rope, keys_c, keys_r, lw['c_w_uv'], None).reshape(bsz, L, C_WIDTH)
    mix = jnp.concatenate([ya, yb, yc], axis=-1)
    x = x + jnp.einsum('ble,ed->bld', mix, lw['w_out'])
    h2 = rms_norm(x, lw['mlp_norm_w'])
    hid = jnp.square(jax.nn.relu(jnp.einsum('bld,df->blf', h2, lw['w_up'])))
    x = x + jnp.einsum('blf,fd->bld', hid, lw['w_down'])
    return x, (ckv, kr, s_new, buf_new, va)


def setup_inputs(seed: int = 0) -> dict:
    key = jax.random.key(seed)
    ks = jax.random.split(key, 32)

    def nrm(i, shape, scale):
        return scale * jax.random.normal(ks[i], shape, F32)

    dt = jnp.exp(jax.random.uniform(ks[14], (DEPTH, B_HEADS), F32, math.log(1e-3), math.log(1e-1)))
    return {
        'x_prompt': nrm(0, (BATCH, SEQ, D_MODEL), 1.0),
        'x_sample': nrm(1, (DEC_BATCH, DEC_SEQ, D_MODEL), 1.0),
        'cache_mla_ckv': nrm(2, (DEPTH, DEC_BATCH, PAST_LEN, KV_LORA), 1.0),
        'cache_mla_krope': nrm(3, (DEPTH, DEC_BATCH, PAST_LEN, C_ROPE), 1.0),
        'state_gdn': nrm(4, (DEPTH, DEC_BATCH, B_HEADS, B_DK, B_DV), 0.1),
        'state_gdn_conv': nrm(5, (DEPTH, DEC_BATCH, B_CONV - 1, B_QKV), 1.0),
        'attn_norm_w': 1.0 + nrm(6, (DEPTH, D_MODEL), 0.1),
        'w_in': nrm(7, (DEPTH, D_MODEL, IN_COLS), D_MODEL ** -0.5),
        'a_ln_w': 1.0 + nrm(8, (DEPTH, A_WIDTH), 0.1),
        'a_ln_b': nrm(9, (DEPTH, A_WIDTH), 0.02),
        'a_w_s': nrm(10, (DEPTH, A_GROUPS, A_CHUNK, A_CHUNK), A_CHUNK ** -0.5),
        'a_b_s': 1.0 + nrm(11, (DEPTH, A_GROUPS, A_CHUNK), 0.1),
        'b_conv_w': nrm(12, (DEPTH, B_CONV, B_QKV), B_CONV ** -0.5),
        'b_a_log': jnp.log(jax.random.uniform(ks[13], (DEPTH, B_HEADS), F32, 1.0, 16.0)),
        'b_dt_bias': dt + jnp.log(-jnp.expm1(-dt)),
        'b_norm_w': 1.0 + nrm(15, (DEPTH, B_DV), 0.1),
        'c_q_norm_w': 1.0 + nrm(16, (DEPTH, Q_LORA), 0.1),
        'c_w_uq': nrm(17, (DEPTH, Q_LORA, C_HEADS, C_NOPE + C_ROPE), Q_LORA ** -0.5),
        'c_kv_norm_w': 1.0 + nrm(18, (DEPTH, KV_LORA), 0.1),
        'c_w_uk': nrm(19, (DEPTH, KV_LORA, C_HEADS, C_NOPE), KV_LORA ** -0.5),
        'c_w_uv': nrm(20, (DEPTH, KV_LORA, C_HEADS, C_VDIM), KV_LORA ** -0.5),
        'w_out': nrm(21, (DEPTH, MIX_WIDTH, D_MODEL), MIX_WIDTH ** -0.5),
        'mlp_norm_w': 1.0 + nrm(22, (DEPTH, D_MODEL), 0.1),
        'w_up': nrm(23, (DEPTH, D_MODEL, D_FF), D_MODEL ** -0.5),
        'w_down': nrm(24, (DEPTH, D_FF, D_MODEL), D_FF ** -0.5),
        'final_norm_w': 1.0 + nrm(25, (D_MODEL,), 0.1),
    }


def reference(x_prompt, x_sample, cache_mla_ckv, cache_mla_krope, state_gdn, state_gdn_conv,
              attn_norm_w, w_in, a_ln_w, a_ln_b, a_w_s, a_b_s, b_conv_w, b_a_log, b_dt_bias, b_norm_w,
              c_q_norm_w, c_w_uq, c_kv_norm_w, c_w_uk, c_w_uv, w_out, mlp_norm_w, w_up, w_down, final_norm_w):
    pos_p = jnp.arange(x_prompt.shape[1], dtype=jnp.int32)
    pos_s = PAST_LEN + jnp.arange(x_sample.shape[1], dtype=jnp.int32)
    hp, hs = x_prompt, x_sample
    sp_list, ss_list = [], []
    for l in range(DEPTH):
        lw = {'attn_norm_w': attn_norm_w[l], 'w_in': w_in[l], 'a_ln_w': a_ln_w[l], 'a_ln_b': a_ln_b[l],
              'a_w_s': a_w_s[l], 'a_b_s': a_b_s[l], 'b_conv_w': b_conv_w[l], 'b_a_log': b_a_log[l],
              'b_dt_bias': b_dt_bias[l], 'b_norm_w': b_norm_w[l], 'c_q_norm_w': c_q_norm_w[l],
              'c_w_uq': c_w_uq[l], 'c_kv_norm_w': c_kv_norm_w[l], 'c_w_uk': c_w_uk[l], 'c_w_uv': c_w_uv[l],
              'w_out': w_out[l], 'mlp_norm_w': mlp_norm_w[l], 'w_up': w_up[l], 'w_down': w_down[l]}
        hp, sp = layer(hp, pos_p, lw, None)
        hs, ss = layer(hs, pos_s, lw, (cache_mla_ckv[l], cache_mla_krope[l], state_gdn[l], state_gdn_conv[l]))
        sp_list.append(sp)
        ss_list.append(ss)
    y_prompt = rms_norm(hp, final_norm_w)
    y_sample = rms_norm(hs, final_norm_w)

    def stack(lst, i):
        return jnp.stack([s[i] for s in lst], axis=0)

    return (y_prompt, y_sample,
            stack(sp_list, 0), stack(sp_list, 1), stack(sp_list, 2), stack(sp_list, 3),
            stack(ss_list, 0), stack(ss_list, 1), stack(ss_list, 2), stack(ss_list, 3), stack(ss_list, 4))
```

```python
import os
import numpy as np
import concourse.bass as bass
import concourse.mybir as mybir
from concourse.bass_utils import run_bass_kernel_spmd

F32 = mybir.dt.float32
BF16 = mybir.dt.bfloat16
AF = mybir.ActivationFunctionType
ALU = mybir.AluOpType
AX = mybir.AxisListType

D = 2048
SEQ = 4096
TP = 512
NTILE = SEQ // TP
TS = 32
PAST = 2048
NTOK = SEQ + TS
C_SCALE = 192 ** -0.5
EPS = 1e-6
SEM_LIMIT = 30000
NT_RUN = int(os.environ.get("MK_NT_RUN", NTILE))


class Buf:
    __slots__ = ("name", "w", "r", "excl")

    def __init__(self, name="", excl=False):
        self.name = name
        self.w = None
        self.r = {}
        self.excl = excl


class Tl:
    __slots__ = ("ap", "b")

    def __init__(self, ap, b):
        self.ap = ap
        self.b = b

    def __getitem__(self, k):
        return Tl(self.ap[k], self.b)


class Sched:
    def __init__(self, nc, n_dma_sems=10):
        self.nc = nc
        self.eng = {"pe": nc.tensor, "dve": nc.vector, "act": nc.scalar,
                    "pool": nc.gpsimd, "sp": nc.sync}
        self.sem = {}
        self.cnt = {}
        self.nsem = 0
        for e in self.eng:
            self._new_sem(e)
        self.waited = {}
        self.pending = {e: False for e in self.eng}
        self.dma_pool = {}
        for q in ("sp", "pool"):
            self.dma_pool[q] = [[self._alloc(f"d{q}{i}"), 0] for i in range(n_dma_sems)]
        self.dma_rr = {"sp": 0, "pool": 0}
        self.n_ins = 0

    def _alloc(self, name):
        self.nsem += 1
        return self.nc.alloc_semaphore(name=f"{name}_{self.nsem}")

    def _new_sem(self, e):
        self.sem[e] = self._alloc(f"s{e}")
        self.cnt[e] = 0

    def _wait(self, e, tok):
        sem, val = tok
        key = (e, id(sem))
        if self.waited.get(key, 0) >= val:
            return
        self.eng[e].wait_ge(sem, val)
        self.waited[key] = val

    def _deps(self, e, reads, writes):
        toks = []
        for b in reads:
            if b.w is not None:
                toks.append(b.w)
            if b.excl:
                toks.extend(t for t in b.r.values() if t[0] is not self.sem[e])
        for b in writes:
            if b.w is not None:
                toks.append(b.w)
            toks.extend(b.r.values())
        for tok in toks:
            if e == "pe" and tok[0] is self.sem["pe"]:
                continue
            self._wait(e, tok)

    def _mark(self, tok, reads, writes):
        for b in reads:
            cur = b.r.get(id(tok[0]))
            if cur is None or cur[1] < tok[1]:
                b.r[id(tok[0])] = tok
        for b in writes:
            b.w = tok
            b.r = {}

    def op(self, e, fn, reads=(), writes=(), sig=True):
        self._deps(e, reads, writes)
        if sig and self.cnt[e] >= SEM_LIMIT and not self.pending[e]:
            self._new_sem(e)
        ins = fn()
        self.n_ins += 1
        if sig:
            ins.then_inc(self.sem[e], 1)
            self.cnt[e] += 1
            tok = (self.sem[e], self.cnt[e])
            self.pending[e] = False
        else:
            tok = (self.sem[e], self.cnt[e] + 1)
            self.pending[e] = True
        self._mark(tok, reads, writes)
        return ins

    def dma(self, q, out, in_, reads=(), writes=()):
        self._deps(q, reads, writes)
        pool = self.dma_pool[q]
        i = self.dma_rr[q]
        self.dma_rr[q] = (i + 1) % len(pool)
        ent = pool[i]
        if ent[1] > 0:
            self._wait(q, (ent[0], ent[1]))
        if ent[1] >= SEM_LIMIT:
            ent[0] = self._alloc(f"d{q}")
            ent[1] = 0
        ins = self.eng[q].dma_start(out=out, in_=in_)
        ins.then_inc(ent[0], 16)
        ent[1] += 16
        self.n_ins += 1
        tok = (ent[0], ent[1])
        self._mark(tok, reads, writes)
        return ins

    def finish(self):
        for q, pool in self.dma_pool.items():
            for sem, val in pool:
                if val > 0:
                    self._wait("sp", (sem, val))
        for e in ("pe", "dve", "act", "pool"):
            if self.cnt[e] > 0:
                self._wait("sp", (self.sem[e], self.cnt[e]))


class Ring:
    def __init__(self, tiles):
        self.tiles = tiles
        self.i = 0

    def next(self):
        t = self.tiles[self.i]
        self.i = (self.i + 1) % len(self.tiles)
        return t


def build_program():
    nc = bass.Bass("TRN2", target_bir_lowering=False)
    S = Sched(nc)

    def din(name, shape):
        return nc.dram_tensor(name, list(shape), F32, kind="ExternalInput").ap()

    def dout(name, shape):
        return nc.dram_tensor(name, list(shape), F32, kind="ExternalOutput").ap()

    xT_d = din("xT", [D, SEQ])
    xsT_d = din("xsT", [D, TS])
    ckvpT_d = din("ckvpT", [2, 256, PAST])
    ckvp_d = din("ckvp", [2, PAST, 256])
    krpT_d = din("krpT", [2, 64, PAST])
    s0_d = din("s0", [2, 128, 512])
    convp_d = din("convp", [2, 128, 36])
    w_in_d = din("w_in_t", [2, 16, 128, 4096])
    w_out_d = din("w_out_t", [2, 8, 128, 4096])
    w_up_d = din("w_up_t", [2, 32, 128, 4096])
    w_down_d = din("w_down_t", [2, 4, 8, 128, 4096])
    w_uq_d = din("w_uq_t", [2, 2, 128, 4096])
    w_ukv_d = din("w_ukv_t", [2, 128, 4096])
    cvec_d = din("cvec", [2, 128, 128])
    wsT_d = din("wsT", [2, 128, 512])
    bs_d = din("bs", [2, 1, 512])
    cos_d = din("cos2", [64, NTOK])
    sin_d = din("sin2", [64, NTOK])
    cst_d = din("cst", [128, 640])

    yT_d = dout("yT", [D, SEQ])
    ysT_d = dout("ysT", [D, TS])
    ckv_o = dout("ckv_o", [2, 256, SEQ])
    kr_o = dout("kr_o", [2, 64, SEQ])
    gs_o = dout("gs_o", [2, 128, 512])
    conv_o = dout("conv_o", [2, 128, 36])
    ckvs_o = dout("ckvs_o", [2, 256, TS])
    krs_o = dout("krs_o", [2, 64, TS])
    gss_o = dout("gss_o", [2, 128, 512])
    convs_o = dout("convs_o", [2, 128, 36])
    vns_o = dout("vns_o", [2, 512, TS])
    xres_d = nc.dram_tensor("xres", [D, NTOK], F32, kind="Internal").ap()
    Bxres = [Buf(f"xres{i}") for i in range(NTILE + 1)]
    NWG = 2 * (16 + 8 + 32 + 32 + 3)
    wscr_d = nc.dram_tensor("wscr", [NWG, 128, 4096], BF16, kind="Internal").ap()
    wscr_idx = {}

    def sb(name, shape, dt):
        return Tl(nc.alloc_sbuf_tensor("sb_" + name, list(shape), dt).ap(), Buf(name))

    xT = sb("xT", [128, 16, TP], F32)
    actb = sb("actb", [128, 16, TP], BF16)
    mix = Tl(actb.ap, actb.b)
    arenaM = nc.alloc_sbuf_tensor("sb_arenaM", [128, 4 * 4 * TP // 2], F32).ap()
    arenaMb = arenaM.bitcast(BF16)
    arenaG = nc.alloc_sbuf_tensor("sb_arenaG", [128, 4352], F32).ap()
    arenaGb = arenaG.bitcast(BF16)

    def carve(ap2d, a, name):
        return Tl(ap2d.rearrange("p (a t) -> p a t", a=a), Buf(name))

    hid = carve(arenaMb[:, 0:16 * TP], 16, "hid")
    ckvT = sb("ckvT", [128, 2, SEQ], BF16)
    krT = sb("krT", [64, SEQ], BF16)
    Vc = sb("Vc", [128, 32, 256], BF16)
    WS = Ring([sb(f"wslot{i}", [128, 4096], BF16) for i in range(3)])
    R32 = Ring([sb(f"r32_{i}", [128, 512], F32) for i in range(6)])
    R16 = Ring([sb(f"r16_{i}", [128, 512], BF16) for i in range(6)])
    CIN = Ring([sb(f"cin{i}", [128, TP + 3], F32) for i in range(2)])
    uT = carve(arenaGb[:, 0:2048], 4, "uT")
    v32 = carve(arenaG[:, 1024:3072], 4, "v32")
    vnb = carve(arenaGb[:, 6144:8192], 4, "vnb")
    qh = carve(arenaMb[:, 0:2048], 4, "qh")
    kh = carve(arenaMb[:, 2048:4096], 4, "kh")
    vT = carve(arenaMb[:, 4096:6144], 4, "vT")
    sz = carve(arenaMb[:, 6144:8192], 4, "sz")
    cqn = sb("cqn", [128, 4, TP], BF16)
    ckv32 = sb("ckv32", [128, 2, TP], F32)
    betaT = sb("betaT", [4, TP], F32)
    gT = sb("gT", [4, TP], F32)
    cosT = sb("cosT", [64, TP], F32)
    sinT = sb("sinT", [64, TP], F32)
    cst = sb("cst", [128, 640], F32)
    cvec = sb("cvec", [128, 128], F32)
    wsf = sb("wsf", [128, 4, 128], F32)
    wsm = sb("wsm", [128, 4, 128], BF16)
    bsf = sb("bsf", [1, 512], F32)
    bsb = sb("bsb", [1, 512], BF16)
    identb = sb("identb", [128, 128], BF16)
    onesb = sb("onesb", [128, 128], BF16)
    zerosb = sb("zerosb", [128, 128], BF16)
    onesf = sb("onesf", [128, 128], F32)
    incl4 = sb("incl4", [128, 4, 128], BF16)
    strict4 = sb("strict4", [128, 4, 128], BF16)
    ident4 = sb("ident4", [128, 4, 128], BF16)
    elast8 = sb("elast8", [128, 8], F32)
    gab = sb("gab", [128, 8], F32)
    vnew2 = sb("vnew2", [128, 512], BF16)
    incl32 = sb("incl32", [32, 4, 32], BF16)
    strict32 = sb("strict32", [32, 4, 32], BF16)
    ident32 = sb("ident32", [32, 4, 32], BF16)
    epsc = sb("epsc", [128, 1], F32)
    eps5 = sb("eps5", [128, 1], F32)
    onec = sb("onec", [128, 1], F32)
    nexpa = sb("nexpa", [4, 1], F32)
    Sst = sb("Sst", [128, 512], F32)
    Sbf = sb("Sbf", [128, 512], BF16)
    ctail_p = sb("ctail_p", [128, 36], F32)
    ctail_s = sb("ctail_s", [128, 36], F32)
    bg = sb("bg", [128, 8], F32)
    gcc = sb("gcc", [128, 4], F32)
    egc = sb("egc", [128, 4], F32)
    elast = sb("elast", [128, 4], F32)
    ssn = sb("ssn", [128, 4], F32)
    Dm = Tl(arenaG[:, 0:512], Buf("Dm"))
    og = Tl(arenaG[:, 512:1024], Buf("og"))
    gb = {n: Tl(arenaGb[:, 2048 + i * 512:2048 + (i + 1) * 512], Buf("g_" + n)) for i, n in enumerate(
          ["qkT", "Xa", "Xb", "Ya", "Yb", "R", "RT", "vtok", "kg", "kdec", "nXk", "vnew", "on"])}
    G1_bufs = [uT, v32, vnb]
    G2_bufs = [Dm, og] + list(gb.values())
    M_bufs = [qh, kh, vT, sz]

    def fence(srcs, dsts):
        toks = {}
        for s_ in srcs:
            for tok in ([s_.b.w] if s_.b.w is not None else []) + list(s_.b.r.values()):
                k = id(tok[0])
                if k not in toks or toks[k][1] < tok[1]:
                    toks[k] = tok
        for d in dsts:
            for k, tok in toks.items():
                cur = d.b.r.get(k)
                if cur is None or cur[1] < tok[1]:
                    d.b.r[k] = tok
    qn = sb("qn", [128, TP], BF16)
    qr = sb("qr", [64, TP], BF16)
    qlat = sb("qlat", [128, 2, TP], BF16)
    OT = sb("OT", [128, 2, TP], BF16)

    PSall = [Tl(nc.alloc_psum_tensor(f"psb{i}", [128, 512], F32).ap(), Buf(f"ps{i}", excl=True)) for i in range(8)]
    class RingSel:
        def __init__(self, ring):
            self.ring = ring

        def next(self):
            return self.ring.next()

    PS_ALL, PS_A, PS_B = Ring(PSall[0:5]), Ring(PSall[0:3]), Ring(PSall[3:5])
    PS = RingSel(PS_ALL)

    def run_interleaved(pairs):
        live = list(pairs)
        while live:
            for item in list(live):
                PS.ring = item[1]
                try:
                    for _ in range(item[2]):
                        next(item[0])
                except StopIteration:
                    live.remove(item)
        PS.ring = PS_ALL
    ACC = PSall[5:8]

    def bfv(t):
        return Tl(t.ap.bitcast(BF16), t.b)

    def v3(t, P, A, Bn):
        return Tl(t.ap[0:P, 0:A * Bn].rearrange("p (a b) -> p a b", a=A), t.b)

    def bc(t, n):
        return Tl(t.ap.unsqueeze(2).to_broadcast([t.ap.shape[0], t.ap.shape[1], n]), t.b)

    def _rb(xs):
        return [x.b for x in xs if isinstance(x, Tl)]

    def _a(x):
        return x.ap if isinstance(x, Tl) else x

    def mm(out, lhsT, rhs, start, stop, sig=None):
        S.op("pe", lambda: nc.tensor.matmul(out.ap, lhsT=lhsT.ap, rhs=rhs.ap, start=start, stop=stop),
             reads=[lhsT.b, rhs.b], writes=[out.b], sig=(stop if sig is None else sig))

    def tr(out, in_, ident, sig=True):
        S.op("pe", lambda: nc.tensor.transpose(out.ap, in_.ap, ident.ap),
             reads=[in_.b, ident.b], writes=[out.b], sig=sig)

    def act(out, in_, func, bias=None, scale=1.0):
        kw = {}
        if bias is not None:
            kw["bias"] = _a(bias)
        S.op("act", lambda: nc.scalar.activation(out=out.ap, in_=in_.ap, func=func, scale=_a(scale), **kw),
             reads=[in_.b] + _rb([bias, scale]), writes=[out.b])

    def tt(out, in0, in1, op, e="dve"):
        S.op(e, lambda: S.eng[e].tensor_tensor(out=out.ap, in0=in0.ap, in1=in1.ap, op=op),
             reads=[in0.b, in1.b], writes=[out.b])

    def ts(out, in0, s1, s2, op0, op1=None, e="dve"):
        kw = {} if op1 is None else {"op1": op1}
        S.op(e, lambda: S.eng[e].tensor_scalar(out=out.ap, in0=in0.ap, scalar1=_a(s1), scalar2=_a(s2), op0=op0, **kw),
             reads=[in0.b] + _rb([s1, s2]), writes=[out.b])

    def stt(out, in0, scalar, in1, op0, op1, e="dve"):
        S.op(e, lambda: S.eng[e].scalar_tensor_tensor(out=out.ap, in0=in0.ap, scalar=_a(scalar), in1=in1.ap, op0=op0, op1=op1),
             reads=[in0.b, in1.b] + _rb([scalar]), writes=[out.b])

    def cp(out, in_, e="dve"):
        if e == "act":
            act(out, in_, AF.Copy)
        else:
            S.op(e, lambda: S.eng[e].tensor_copy(out=out.ap, in_=in_.ap), reads=[in_.b], writes=[out.b])

    def recip(out, in_):
        S.op("dve", lambda: nc.vector.reciprocal(out=out.ap, in_=in_.ap), reads=[in_.b], writes=[out.b])

    def memset(out, val, e="dve"):
        S.op(e, lambda: S.eng[e].memset(out.ap, val), writes=[out.b])

    def dma(q, out, in_, reads=(), writes=()):
        S.dma(q, _a(out), _a(in_), reads=list(reads) + _rb([in_]), writes=list(writes) + _rb([out]))

    dma("pool", cst, cst_d)
    c_id, c_incl, c_strict, c_gm = (cst[:, i * 128:(i + 1) * 128] for i in range(4))
    cp(identb, c_id)
    memset(onesb, 1.0)
    memset(zerosb, 0.0)
    memset(onesf, 1.0)
    memset(epsc, EPS)
    memset(eps5, 1e-5)
    memset(onec, 1.0)
    for dst, src in ((incl32, c_incl), (strict32, c_strict), (ident32, c_id)):
        cp(dst, Tl(src.ap[0:32, 0:32].unsqueeze(1).to_broadcast([32, 4, 32]), src.b))
    for dst, src in ((incl4, c_incl), (strict4, c_strict), (ident4, c_id)):
        cp(dst, Tl(src.ap.unsqueeze(1).to_broadcast([128, 4, 128]), src.b))

    def weight_keys(l):
        ks = [(("in", l, g), w_in_d[l, g]) for g in range(16)]
        ks += [(("uq", l, 0), w_uq_d[l, 0]), (("uq", l, 1), w_uq_d[l, 1]), (("ukv", l), w_ukv_d[l])]
        ks += [(("out", l, g), w_out_d[l, g]) for g in range(8)]
        for fc in range(4):
            ks += [(("up", l, fc * 8 + g), w_up_d[l, fc * 8 + g]) for g in range(8)]
            ks += [(("down", l, fc, mp), w_down_d[l, fc, mp]) for mp in range(8)]
        return ks

    def convert_w(items):
        for key, src in items:
            if key in wscr_idx:
                continue
            wscr_idx[key] = (len(wscr_idx), Buf("wscr"))
            i, b_ = wscr_idx[key]
            dma("pool", wscr_d[i], src, writes=[b_])

    def load_w(src, key):
        if key not in wscr_idx:
            convert_w([(key, src)])
        i, b_ = wscr_idx[key]
        slot = WS.next()
        dma("sp", slot, wscr_d[i], reads=[b_])
        return slot

    def rmsnorm_x(wcol, T, out3, f32_out=False):
        pss = PS.next()
        for c in range(16):
            sq = R16.next()
            act(sq[:, 0:T], xT[:, c, 0:T], AF.Square)
            mm(pss[:, 0:T], onesb, sq[:, 0:T], c == 0, c == 15, sig=True)
        rs = R32.next()
        act(rs[:, 0:T], pss[:, 0:T], AF.Sqrt, bias=epsc, scale=1.0 / D)
        recip(rs[:, 0:T], rs[:, 0:T])
        for c in range(16):
            stt(out3[:, c, 0:T], xT[:, c, 0:T], cvec[:, wcol + c:wcol + c + 1], rs[:, 0:T], ALU.mult, ALU.mult)

    deferred = []

    def flush_deferred(keep=0):
        for ent in deferred:
            ent[1] += 1
        while deferred and deferred[0][1] > keep:
            deferred.pop(0)[0]()

    def proj(slot, nk, mtiles, rhs_of_k, T, evac, acc_ps=None, k0=0, ktot=None):
        w = slot.ap.rearrange("p (k n) -> p k n", k=nk)
        ktot = nk if ktot is None else ktot
        for (co, M, tag, idx) in mtiles:
            ps = PS.next() if acc_ps is None else acc_ps
            for k in range(nk):
                mm(ps[0:M, 0:T], Tl(w[:, k, co:co + M], slot.b), rhs_of_k(k0 + k),
                   (k0 + k) == 0, (k0 + k) == ktot - 1)
            flush_deferred(keep=2)
            if k0 + nk == ktot:
                evac(tag, idx, ps)

    WIN_GROUPS = []
    for tg in ("u", "v", "q", "k", "vv", "z", "cq"):
        WIN_GROUPS.append([(0, 128, tg, 0), (128, 128, tg, 1)])
        WIN_GROUPS.append([(0, 128, tg, 2), (128, 128, tg, 3)])
    WIN_GROUPS.append([(0, 128, "ckv", 0), (128, 128, "ckv", 1)])
    WIN_GROUPS.append([(0, 64, "kr", 0), (64, 64, "krsw", 0), (128, 4, "beta", 0), (132, 4, "alpha", 0)])

    def process_tile(l, kind, ti):
        if kind == "p":
            T, C, K, tok0, key0, hist_kt, pcol = TP, 64, 5, ti * TP, ti * TP, (ti * TP) // 128, ti * TP
            ctail, m_incl, m_strict, m_ident = ctail_p, None, None, None
            ckv_out, kr_out = ckv_o[l], kr_o[l]
            xsrc = xT_d if l == 0 else xres_d
            xcols = slice(tok0, tok0 + T)
            ocols = slice(tok0, tok0 + T)
            bx = Bxres[ti]
        else:
            T, C, K, tok0, key0, hist_kt, pcol = TS, 32, 4, 0, PAST, PAST // 128, SEQ
            ctail, m_incl, m_strict, m_ident = ctail_s, incl32, strict32, ident32
            ckv_out, kr_out = ckvs_o[l], krs_o[l]
            xsrc = xsT_d if l == 0 else xres_d
            xcols = slice(0, TS) if l == 0 else slice(SEQ, SEQ + TS)
            ocols = slice(0, TS)
            bx = Bxres[NTILE]
        nblk = (T + 127) // 128

        dma("pool", xT[:, :, 0:T], xsrc.rearrange("(c p) t -> p c t", p=128)[:, :, xcols],
            reads=[bx] if l == 1 else [])
        dma("pool", cosT[:, 0:T], cos_d[:, pcol:pcol + T])
        dma("pool", sinT[:, 0:T], sin_d[:, pcol:pcol + T])

        if kind == "p" and l == 0:
            if ti == 0:
                convert_w(weight_keys(0))
            else:
                k1 = weight_keys(1)
                per = (len(k1) + NT_RUN - 2) // max(1, NT_RUN - 1)
                convert_w(k1[(ti - 1) * per:ti * per])
        rmsnorm_x(0, T, actb)

        st8 = {}

        def evac_in(tag, m, ps):
            if tag == "u":
                act(uT[:, m, 0:T], ps[:, 0:T], AF.Gelu)
            elif tag == "v":
                act(v32[:, m, 0:T], ps[:, 0:T], AF.Gelu)
            elif tag in ("q", "k", "vv"):
                ch = {"q": 0, "k": 4, "vv": 8}[tag] + m
                cin = CIN.next()
                cp(cin[:, 0:3], ctail[:, ch * 3:ch * 3 + 3])
                act(cin[:, 3:3 + T], ps[:, 0:T], AF.Copy)
                acc = R32.next()
                cw = 40 + ch * 4
                ts(acc[:, 0:T], cin[:, 0:T], cvec[:, cw:cw + 1], None, ALU.mult)
                for i in range(1, 4):
                    stt(acc[:, 0:T], cin[:, i:i + T], cvec[:, cw + i:cw + i + 1], acc[:, 0:T], ALU.mult, ALU.add)
                cp(ctail[:, ch * 3:ch * 3 + 3], cin[:, T:T + 3])
                if tag == "vv":
                    act(vT[:, m, 0:T], acc[:, 0:T], AF.Silu)
                else:
                    dst = qh if tag == "q" else kh
                    act(dst[:, m, 0:T], acc[:, 0:T], AF.Silu)
                    sq = R16.next()
                    act(sq[:, 0:T], dst[:, m, 0:T], AF.Square)

                    def fin(dst=dst, sq=sq, tag=tag, m=m):
                        pss = PS.next()
                        mm(pss[:, 0:T], onesb, sq[:, 0:T], True, True)
                        rs = R32.next()
                        act(rs[:, 0:T], pss[:, 0:T], AF.Sqrt, bias=epsc, scale=1.0)
                        recip(rs[:, 0:T], rs[:, 0:T])
                        if tag == "q":
                            stt(dst[:, m, 0:T], dst[:, m, 0:T], float(128 ** -0.5), rs[:, 0:T], ALU.mult, ALU.mult)
                        else:
                            tt(dst[:, m, 0:T], dst[:, m, 0:T], rs[:, 0:T], ALU.mult)

                    deferred.append([fin, 0])
            elif tag == "z":
                act(sz[:, m, 0:T], ps[:, 0:T], AF.Silu)
            elif tag == "cq":
                act(cqn[:, m, 0:T], ps[:, 0:T], AF.Copy)
                if m == 3:
                    pss = PS.next()
                    for c in range(4):
                        sq = R16.next()
                        act(sq[:, 0:T], cqn[:, c, 0:T], AF.Square)
                        mm(pss[:, 0:T], onesb, sq[:, 0:T], c == 0, c == 3, sig=True)
                    rs = R32.next()
                    act(rs[:, 0:T], pss[:, 0:T], AF.Sqrt, bias=epsc, scale=1.0 / 512)
                    recip(rs[:, 0:T], rs[:, 0:T])
                    for c in range(4):
                        stt(cqn[:, c, 0:T], cqn[:, c, 0:T], cvec[:, 88 + c:89 + c], rs[:, 0:T], ALU.mult, ALU.mult)
            elif tag == "ckv":
                act(ckv32[:, m, 0:T], ps[:, 0:T], AF.Copy)
                if m == 1:
                    pss = PS.next()
                    for c in range(2):
                        sq = R16.next()
                        act(sq[:, 0:T], ckv32[:, c, 0:T], AF.Square)
                        mm(pss[:, 0:T], onesb, sq[:, 0:T], c == 0, c == 1, sig=True)
                    rs = R32.next()
                    act(rs[:, 0:T], pss[:, 0:T], AF.Sqrt, bias=epsc, scale=1.0 / 256)
                    recip(rs[:, 0:T], rs[:, 0:T])
                    for c in range(2):
                        stt(ckv32[:, c, 0:T], ckv32[:, c, 0:T], cvec[:, 92 + c:93 + c], rs[:, 0:T], ALU.mult, ALU.mult)
                    dma("pool", ckv_out.rearrange("(c p) t -> p c t", p=128)[:, :, ocols], ckv32[:, :, 0:T])
                    cp(ckvT[:, :, key0:key0 + T], ckv32[:, :, 0:T], e="act")
                    def fin_v():
                        for kb in range(nblk):
                            nk = min(128, T - kb * 128)
                            pb = bfv(PS.next())
                            for c in range(2):
                                tr(pb[0:nk, c * 128:(c + 1) * 128],
                                   ckvT[:, c, key0 + kb * 128:key0 + kb * 128 + nk], identb, sig=(c == 1))
                            cp(Vc[0:nk, key0 // 128 + kb, :], pb[0:nk, 0:256])

                    deferred.append([fin_v, 0])
            elif tag == "kr":
                st8["kr"] = ps
            elif tag == "krsw":
                t1 = R32.next()
                t2 = R32.next()
                tt(t1[0:64, 0:T], st8["kr"][0:64, 0:T], cosT[:, 0:T], ALU.mult)
                tt(t2[0:64, 0:T], ps[0:64, 0:T], sinT[:, 0:T], ALU.mult)
                tt(t1[0:64, 0:T], t1[0:64, 0:T], t2[0:64, 0:T], ALU.add)
                dma("pool", kr_out[:, ocols], t1[0:64, 0:T])
                cp(krT[:, key0:key0 + T], t1[0:64, 0:T], e="act")
            elif tag == "beta":
                act(betaT[:, 0:T], ps[0:4, 0:T], AF.Sigmoid)
            elif tag == "alpha":
                t1 = R32.next()
                act(t1[0:4, 0:T], ps[0:4, 0:T], AF.Exp, bias=cvec[0:4, 111:112])
                act(t1[0:4, 0:T], t1[0:4, 0:T], AF.Ln, bias=onec[0:4, :])
                ts(gT[:, 0:T], t1[0:4, 0:T], nexpa[:, 0:1], None, ALU.mult)

        for g in range(16):
            slot = load_w(w_in_d[l, g], ("in", l, g))
            proj(slot, 16, WIN_GROUPS[g], lambda k: actb[:, k, 0:T], T, evac_in)
        flush_deferred()

        psm = PS.next()
        for c in range(4):
            mm(psm[:, 0:T], onesf, v32[:, c, 0:T], c == 0, c == 3)
        for c in range(4):
            stt(v32[:, c, 0:T], psm[:, 0:T], -1.0 / 512, v32[:, c, 0:T], ALU.mult, ALU.add)
        psv = PS.next()
        for c in range(4):
            sq = R16.next()
            act(sq[:, 0:T], v32[:, c, 0:T], AF.Square)
            mm(psv[:, 0:T], onesb, sq[:, 0:T], c == 0, c == 3, sig=True)
        rs = R32.next()
        act(rs[:, 0:T], psv[:, 0:T], AF.Sqrt, bias=eps5, scale=1.0 / 512)
        recip(rs[:, 0:T], rs[:, 0:T])
        for c in range(4):
            tt(v32[:, c, 0:T], v32[:, c, 0:T], rs[:, 0:T], ALU.mult)
            ts(v32[:, c, 0:T], v32[:, c, 0:T], cvec[:, 32 + c:33 + c], cvec[:, 36 + c:37 + c], ALU.mult, ALU.add)
        if kind == "s":
            dma("pool", vns_o[l].rearrange("(c p) t -> p c t", p=128), v32[:, :, 0:T])
        cp(vnb[:, :, 0:T], v32[:, :, 0:T], e="act")
        lc = min(T, 128)
        for g in range(4):
            pb = bfv(PS.next())
            for n in range(nblk):
                tr(pb[0:lc, n * 128:(n + 1) * 128], vnb[:, g, n * 128:n * 128 + lc], identb, sig=(n == nblk - 1))
            vtk = R16.next()
            cp(vtk[0:lc, 0:nblk * 128], pb[0:lc, 0:nblk * 128])
            psf = PS.next()
            for n in range(nblk):
                mm(psf[:, n * 128:n * 128 + lc], vtk[0:lc, n * 128:(n + 1) * 128], wsm[0:lc, g, 0:lc], True, False, sig=False)
                mm(psf[:, n * 128:n * 128 + lc], onesb[0:1, 0:128], bsb[0:1, g * 128:g * 128 + lc], False, True,
                   sig=(n == nblk - 1))
            tt(mix[:, g, 0:T], uT[:, g, 0:T], psf[:, 0:T], ALU.mult)

        G = gb
        fence(G1_bufs, G2_bufs)

        def gdn_gen():
            for c0 in range(0, T, C):
                HC = 4 * C
                p1 = PS.next()
                mm(p1[0:C, 0:4], betaT[0:4, c0:c0 + C], cst[0:4, 0:4], True, True, sig=False)
                mm(p1[0:C, 4:8], gT[0:4, c0:c0 + C], cst[0:4, 0:4], True, True)
                cp(bg[0:C, :], p1[0:C, 0:8])
                yield
                bcol = bg[0:C, 0:4]
                gcol = bg[0:C, 4:8]
                p2 = PS.next()
                mm(p2[0:C, 0:4], c_incl[0:C, 0:C], gcol, True, True, sig=False)
                mm(p2[:, 4:8], onesf[0:C, :], gcol, True, True)
                cp(gcc[0:C, :], p2[0:C, 0:4])
                act(egc[0:C, :], p2[0:C, 0:4], AF.Exp)
                act(elast, p2[:, 4:8], AF.Exp)
                yield
                gtri = R32.next()
                tt(v3(gtri, C, 4, C), m_incl[0:C], bc(gcol, C), ALU.mult)
                p3 = PS.next()
                for h in range(4):
                    mm(p3[0:C, h * C:(h + 1) * C], onesf[0:C, 0:C], gtri[0:C, h * C:(h + 1) * C], True, True, sig=(h == 3))
                Dm3 = v3(Dm, C, 4, C)
                tt(Dm3, v3(p3, C, 4, C), bc(gcc[0:C, :], C), ALU.subtract)
                ts(Dm[0:C, 0:HC], Dm[0:C, 0:HC], 0.0, None, ALU.min)
                act(Dm[0:C, 0:HC], Dm[0:C, 0:HC], AF.Exp)
                tt(Dm3, Dm3, m_incl[0:C], ALU.mult)
                yield
                p4 = PS.next()
                p5 = PS.next()
                for h in range(4):
                    mm(p4[0:C, h * C:(h + 1) * C], kh[:, h, c0:c0 + C], kh[:, h, c0:c0 + C], True, True, sig=False)
                for h in range(4):
                    mm(p5[0:C, h * C:(h + 1) * C], kh[:, h, c0:c0 + C], qh[:, h, c0:c0 + C], True, True, sig=(h == 3))
                tt(v3(G["qkT"], C, 4, C), v3(p5, C, 4, C), Dm3, ALU.mult)
                yield
                dms = R32.next()
                dms3 = v3(dms, C, 4, C)
                tt(dms3, Dm3, m_strict[0:C], ALU.mult)
                tt(dms3, dms3, bc(bcol, C), ALU.mult)
                stt(v3(G["Xa"], C, 4, C), v3(p4, C, 4, C), -1.0, dms3, ALU.mult, ALU.mult)
                yield
                pb = bfv(PS.next())
                for h in range(4):
                    tr(pb[0:C, h * C:(h + 1) * C], G["Xa"][0:C, h * C:(h + 1) * C], identb[0:C, 0:C], sig=(h == 3))
                cp(G["Ya"][0:C, 0:HC], pb[0:C, 0:HC], e="act")
                tt(v3(G["R"], C, 4, C), v3(G["Xa"], C, 4, C), m_ident[0:C], ALU.add)
                yield
                X, Y, Xn, Yn = G["Xa"], G["Ya"], G["Xb"], G["Yb"]
                for sN in range(1, K + 2):
                    need_x = sN < K
                    need_y = sN <= K
                    need_p = sN >= 2
                    if need_x:
                        px = PS.next()
                        for h in range(4):
                            hs = slice(h * C, (h + 1) * C)
                            mm(px[0:C, hs], Y[0:C, hs], X[0:C, hs], True, True, sig=(h == 3))
                    if need_y:
                        py = PS.next()
                        for h in range(4):
                            hs = slice(h * C, (h + 1) * C)
                            mm(py[0:C, hs], X[0:C, hs], Y[0:C, hs], True, True, sig=(h == 3))
                    if need_p:
                        pr = PS.next()
                        for h in range(4):
                            hs = slice(h * C, (h + 1) * C)
                            mm(pr[0:C, hs], Y[0:C, hs], G["R"][0:C, hs], True, True, sig=(h == 3))
                    if need_x:
                        cp(Xn[0:C, 0:HC], px[0:C, 0:HC], e="act")
                    if need_y:
                        cp(Yn[0:C, 0:HC], py[0:C, 0:HC])
                    if need_p:
                        tt(G["R"][0:C, 0:HC], G["R"][0:C, 0:HC], pr[0:C, 0:HC], ALU.add)
                    X, Y, Xn, Yn = Xn, Yn, X, Y
                    yield
                pv = bfv(PS.next())
                for h in range(4):
                    tr(pv[0:C, h * 128:(h + 1) * 128], vT[:, h, c0:c0 + C], identb, sig=(h == 3))
                cp(G["vtok"][0:C, :], pv[0:C, 0:512], e="act")
                pk = bfv(PS.next())
                for h in range(4):
                    tr(pk[0:C, h * 128:(h + 1) * 128], kh[:, h, c0:c0 + C], identb, sig=(h == 3))
                pk3 = Tl(pk.ap[0:C, 0:512].rearrange("p (a b) -> p a b", a=4), pk.b)
                tt(v3(G["kg"], C, 4, 128), pk3, bc(egc[0:C, :], 128), ALU.mult)
                dl = Tl(Dm3.ap[:, :, C - 1], Dm.b)
                tt(v3(G["kdec"], C, 4, 128), pk3, bc(dl, 128), ALU.mult)
                yield
                pxk = PS.next()
                for h in range(4):
                    mm(pxk[:, h * C:(h + 1) * C], G["kg"][0:C, h * 128:(h + 1) * 128], G["R"][0:C, h * C:(h + 1) * C],
                       True, True, sig=(h == 3))
                act(G["nXk"][:, 0:HC], pxk[:, 0:HC], AF.Copy, scale=-1.0)
                yield
                pvn = PS.next()
                for h in range(4):
                    hv = slice(h * 128, (h + 1) * 128)
                    mm(pvn[0:C, hv], G["R"][0:C, h * C:(h + 1) * C], G["vtok"][0:C, hv], True, False, sig=False)
                    mm(pvn[0:C, hv], G["nXk"][:, h * C:(h + 1) * C], Sbf[:, hv], False, True, sig=(h == 3))
                tt(v3(G["vnew"], C, 4, 128), v3(pvn, C, 4, 128), bc(bcol, 128), ALU.mult)
                yield
                po1 = PS.next()
                po2 = PS.next()
                for h in range(4):
                    hv = slice(h * 128, (h + 1) * 128)
                    mm(po1[0:C, hv], qh[:, h, c0:c0 + C], Sbf[:, hv], True, True, sig=False)
                for h in range(4):
                    hv = slice(h * 128, (h + 1) * 128)
                    mm(po2[0:C, hv], G["qkT"][0:C, h * C:(h + 1) * C], G["vnew"][0:C, hv], True, True, sig=(h == 3))
                og3 = v3(og, C, 4, 128)
                tt(og3, v3(po1, C, 4, 128), bc(egc[0:C, :], 128), ALU.mult)
                tt(og[0:C, :], og[0:C, :], po2[0:C, :], ALU.add)
                yield
                pds = PS.next()
                for h in range(4):
                    hv = slice(h * 128, (h + 1) * 128)
                    mm(pds[:, hv], G["kdec"][0:C, hv], G["vnew"][0:C, hv], True, True, sig=(h == 3))
                tt(v3(Sst, 128, 4, 128), v3(Sst, 128, 4, 128), bc(elast, 128), ALU.mult)
                tt(Sst, Sst, pds, ALU.add)
                cp(Sbf, Sst, e="act")
                yield
                sqo = R32.next()
                tt(sqo[0:C, :], og[0:C, :], og[0:C, :], ALU.mult)
                S.op("dve", lambda: nc.vector.reduce_sum(out=ssn.ap[0:C, :], in_=v3(sqo, C, 4, 128).ap, axis=AX.X),
                     reads=[sqo.b], writes=[ssn.b])
                act(ssn[0:C, :], ssn[0:C, :], AF.Sqrt, bias=epsc[0:C, :], scale=1.0 / 128)
                recip(ssn[0:C, :], ssn[0:C, :])
                tt(v3(G["on"], C, 4, 128), og3, bc(ssn[0:C, :], 128), ALU.mult)
                yield
                pt = bfv(PS.next())
                for h in range(4):
                    tr(pt[:, h * C:(h + 1) * C], G["on"][0:C, h * 128:(h + 1) * 128], identb[0:C, 0:C], sig=(h == 3))
                pt3 = Tl(pt.ap[:, 0:HC].rearrange("p (a b) -> p a b", a=4), pt.b)
                stt(mix[:, 4:8, c0:c0 + C], pt3, cvec[:, 112:113], sz[:, :, c0:c0 + C], ALU.mult, ALU.mult)
                yield


        def gdn128_gen():
            SC = 128
            c_same = cst[:, 512:640]
            vnh = [G["vnew"], vnew2]
            memset(G["vnew"][64:128, :], 0.0)
            memset(vnew2[0:64, :], 0.0)
            for s0 in range(0, T, SC):
                cs = slice(s0, s0 + SC)
                p1 = PS.next()
                mm(p1[:, 0:4], betaT[0:4, cs], cst[0:4, 0:4], True, True, sig=False)
                mm(p1[:, 4:8], gT[0:4, cs], cst[0:4, 0:4], True, True)
                cp(bg[:, :], p1[:, 0:8])
                yield
                bcol = bg[:, 0:4]
                gcol = bg[:, 4:8]
                p2 = PS.next()
                mm(p2[:, 0:4], c_incl, gcol, True, True, sig=False)
                ts(gab[:, 0:4], gcol, c_same[:, 0:1], None, ALU.mult)
                ts(gab[:, 4:8], gcol, c_same[:, 64:65], None, ALU.mult)
                mm(p2[:, 8:16], onesf, gab, True, True)
                cp(gcc[:, :], p2[:, 0:4])
                act(egc[:, :], p2[:, 0:4], AF.Exp)
                act(elast8, p2[:, 8:16], AF.Exp)
                yield
                gtri = R32.next()
                tt(v3(gtri, 128, 4, 128), incl4, bc(gcol, 128), ALU.mult)
                p3 = PS.next()
                mm(p3, onesf, gtri, True, True)
                Dm3 = v3(Dm, 128, 4, 128)
                tt(Dm3, v3(p3, 128, 4, 128), bc(gcc, 128), ALU.subtract)
                ts(Dm, Dm, 0.0, None, ALU.min)
                act(Dm, Dm, AF.Exp)
                tt(Dm3, Dm3, incl4, ALU.mult)
                yield
                p4 = PS.next()
                p5 = PS.next()
                for h in range(4):
                    mm(p4[:, h * 128:(h + 1) * 128], kh[:, h, cs], kh[:, h, cs], True, True, sig=False)
                for h in range(4):
                    mm(p5[:, h * 128:(h + 1) * 128], kh[:, h, cs], qh[:, h, cs], True, True, sig=(h == 3))
                tt(G["qkT"], p5, Dm, ALU.mult)
                dms = R32.next()
                dms3 = v3(dms, 128, 4, 128)
                tt(dms3, Dm3, strict4, ALU.mult)
                tt(dms3, dms3, bc(bcol, 128), ALU.mult)
                stt(G["Xa"], p4, -1.0, dms, ALU.mult, ALU.mult)
                yield
                pb = bfv(PS.next())
                for h in range(4):
                    hs = slice(h * 128, (h + 1) * 128)
                    tr(pb[:, hs], G["Xa"][:, hs], identb, sig=(h == 3))
                cp(G["Ya"], pb[:, 0:512], e="act")
                tt(v3(G["R"], 128, 4, 128), v3(G["Xa"], 128, 4, 128), ident4, ALU.add)
                yield
                X, Y, Xn, Yn = G["Xa"], G["Ya"], G["Xb"], G["Yb"]
                for sN in range(1, K + 2):
                    need_x = sN < K
                    need_y = sN <= K
                    need_p = sN >= 2
                    if need_x:
                        px = PS.next()
                        for h in range(4):
                            hs = slice(h * 128, (h + 1) * 128)
                            mm(px[:, hs], Y[:, hs], X[:, hs], True, True, sig=(h == 3))
                    if need_y:
                        py = PS.next()
                        for h in range(4):
                            hs = slice(h * 128, (h + 1) * 128)
                            mm(py[:, hs], X[:, hs], Y[:, hs], True, True, sig=(h == 3))
                    if need_p:
                        pr = PS.next()
                        for h in range(4):
                            hs = slice(h * 128, (h + 1) * 128)
                            mm(pr[:, hs], Y[:, hs], G["R"][:, hs], True, True, sig=(h == 3))
                    if need_x:
                        cp(Xn, px, e="act")
                    if need_y:
                        cp(Yn, py)
                    if need_p:
                        tt(G["R"], G["R"], pr, ALU.add)
                    X, Y, Xn, Yn = Xn, Yn, X, Y
                    yield
                pv = bfv(PS.next())
                for h in range(4):
                    tr(pv[:, h * 128:(h + 1) * 128], vT[:, h, cs], identb, sig=(h == 3))
                cp(G["vtok"], pv[:, 0:512], e="act")
                pk = bfv(PS.next())
                for h in range(4):
                    tr(pk[:, h * 128:(h + 1) * 128], kh[:, h, cs], identb, sig=(h == 3))
                pk3 = Tl(pk.ap[:, 0:512].rearrange("p (a b) -> p a b", a=4), pk.b)
                tt(v3(G["kg"], 128, 4, 128), pk3, bc(egc, 128), ALU.mult)
                for (r0, lastc) in ((0, 63), (64, 127)):
                    dl = Tl(Dm3.ap[r0:r0 + 64, :, lastc], Dm.b)
                    tt(Tl(v3(G["kdec"], 128, 4, 128).ap[r0:r0 + 64], G["kdec"].b),
                       Tl(pk3.ap[r0:r0 + 64], pk.b), bc(dl, 128), ALU.mult)
                yield
                pxk = PS.next()
                pp = PS.next()
                for h in range(4):
                    hs = slice(h * 128, (h + 1) * 128)
                    mm(pxk[:, hs], G["kg"][:, hs], G["R"][:, hs], True, True, sig=False)
                for h in range(4):
                    hs = slice(h * 128, (h + 1) * 128)
                    mm(pp[:, hs], G["R"][:, hs], G["vtok"][:, hs], True, True, sig=(h == 3))
                act(G["nXk"], pxk, AF.Copy, scale=-1.0)
                cp(G["RT"], pp)
                yield
                og3 = v3(og, 128, 4, 128)
                for half, r0 in enumerate((0, 64)):
                    rs_ = slice(r0, r0 + 64)
                    pvn = PS.next()
                    po1 = PS.next()
                    for h in range(4):
                        hs = slice(h * 128, (h + 1) * 128)
                        mm(pvn[:, hs], G["nXk"][:, hs], Sbf[:, hs], True, True, sig=False)
                    for h in range(4):
                        hs = slice(h * 128, (h + 1) * 128)
                        mm(po1[:, hs], qh[:, h, cs], Sbf[:, hs], True, True, sig=(h == 3))
                    tmpv = R32.next()
                    tt(tmpv[rs_, :], pvn[rs_, :], G["RT"][rs_, :], ALU.add)
                    vn_ = vnh[half]
                    tt(Tl(v3(vn_, 128, 4, 128).ap[rs_], vn_.b), Tl(v3(tmpv, 128, 4, 128).ap[rs_], tmpv.b),
                       bc(bg[rs_, 0:4], 128), ALU.mult)
                    tt(Tl(og3.ap[rs_], og.b), Tl(v3(po1, 128, 4, 128).ap[rs_], po1.b), bc(egc[rs_, :], 128), ALU.mult)
                    pds = PS.next()
                    for h in range(4):
                        hs = slice(h * 128, (h + 1) * 128)
                        mm(pds[:, hs], G["kdec"][:, hs], vn_[:, hs], True, True, sig=(h == 3))
                    tt(v3(Sst, 128, 4, 128), v3(Sst, 128, 4, 128), bc(elast8[:, half * 4:half * 4 + 4], 128), ALU.mult)
                    tt(Sst, Sst, pds, ALU.add)
                    cp(Sbf, Sst, e="act")
                    yield
                po2 = PS.next()
                for h in range(4):
                    hs = slice(h * 128, (h + 1) * 128)
                    mm(po2[:, hs], G["qkT"][:, hs], vnh[0][:, hs], True, False, sig=False)
                    mm(po2[:, hs], G["qkT"][:, hs], vnh[1][:, hs], False, True, sig=(h == 3))
                tt(og, og, po2, ALU.add)
                sqo = R32.next()
                tt(sqo, og, og, ALU.mult)
                S.op("dve", lambda: nc.vector.reduce_sum(out=ssn.ap, in_=v3(sqo, 128, 4, 128).ap, axis=AX.X),
                     reads=[sqo.b], writes=[ssn.b])
                act(ssn, ssn, AF.Sqrt, bias=epsc, scale=1.0 / 128)
                recip(ssn, ssn)
                tt(v3(G["on"], 128, 4, 128), og3, bc(ssn, 128), ALU.mult)
                yield
                pt = bfv(PS.next())
                for h in range(4):
                    hs = slice(h * 128, (h + 1) * 128)
                    tr(pt[:, hs], G["on"][:, hs], identb, sig=(h == 3))
                pt3 = Tl(pt.ap[:, 0:512].rearrange("p (a b) -> p a b", a=4), pt.b)
                stt(mix[:, 4:8, cs], pt3, cvec[:, 112:113], sz[:, :, cs], ALU.mult, ALU.mult)
                yield

        def mla_gen():
            uq = [load_w(w_uq_d[l, 0], ("uq", l, 0)), load_w(w_uq_d[l, 1], ("uq", l, 1))]
            ukv = load_w(w_ukv_d[l], ("ukv", l))
            own = []
            if kind == "p":
                for j in range(T // 128):
                    own.append((key0 // 128 + j, 128, j * 128, True))
            else:
                own.append((PAST // 128, TS, 0, False))
            for h in range(8):
                wq = uq[h // 4].ap.rearrange("p (k a n) -> p k a n", k=4, a=4)
                wqb = uq[h // 4].b
                hh = h % 4
                pn = PS.next()
                for kt in range(4):
                    mm(pn[:, 0:T], Tl(wq[:, kt, hh, 0:128], wqb), cqn[:, kt, 0:T], kt == 0, kt == 3)
                cp(qn[:, 0:T], pn[:, 0:T], e="act")
                yield
                pr1 = PS.next()
                for kt in range(4):
                    mm(pr1[0:64, 0:T], Tl(wq[:, kt, hh, 128:192], wqb), cqn[:, kt, 0:T], kt == 0, kt == 3)
                pr2 = PS.next()
                for kt in range(4):
                    mm(pr2[0:64, 0:T], Tl(wq[:, kt, hh, 192:256], wqb), cqn[:, kt, 0:T], kt == 0, kt == 3)
                t1 = R32.next()
                t2 = R32.next()
                tt(t1[0:64, 0:T], pr1[0:64, 0:T], cosT[:, 0:T], ALU.mult)
                tt(t2[0:64, 0:T], pr2[0:64, 0:T], sinT[:, 0:T], ALU.mult)
                tt(qr[:, 0:T], t1[0:64, 0:T], t2[0:64, 0:T], ALU.add)
                yield
                for cc in range(2):
                    pl = PS.next()
                    mm(pl[:, 0:T], ukv[:, h * 256 + cc * 128:h * 256 + (cc + 1) * 128], qn[:, 0:T], True, True)
                    cp(qlat[:, cc, 0:T], pl[:, 0:T], e=("act" if cc == 0 else "dve"))
                    yield
                steps = [(kt, 128, 0, False) for kt in range(hist_kt)] + own
                sts = {}

                def emit_score(sj):
                    ktg_, nk_, q0_, _ = steps[sj]
                    ks = slice(ktg_ * 128, ktg_ * 128 + nk_)
                    st_ = PS.next()
                    mm(st_[0:nk_, q0_:T], ckvT[:, 0, ks], qlat[:, 0, q0_:T], True, False, sig=False)
                    mm(st_[0:nk_, q0_:T], ckvT[:, 1, ks], qlat[:, 1, q0_:T], False, False, sig=False)
                    mm(st_[0:nk_, q0_:T], krT[:, ks], qr[:, q0_:T], False, True)
                    sts[sj] = st_

                emit_score(0)
                for si, (ktg, nk, q0, diag) in enumerate(steps):
                    if si + 1 < len(steps):
                        emit_score(si + 1)
                    st = sts.pop(si)
                    pT = R16.next()
                    act(pT[0:nk, q0:T], st[0:nk, q0:T], AF.Exp, scale=C_SCALE)
                    if diag:
                        memset(pT[64:128, q0:q0 + 64], 0.0)
                    lhs = [Vc[0:nk, ktg, 0:128], Vc[0:nk, ktg, 128:256], onesb[0:nk, :]]
                    for a in range(3):
                        mm(ACC[a][:, q0:T], lhs[a], pT[0:nk, q0:T], si == 0, False, sig=False)
                        yield
                for a in range(3):
                    mm(ACC[a][:, 0:T], zerosb, qlat[:, 0, 0:T], False, True, sig=(a == 2))
                rinv = R32.next()
                recip(rinv[:, 0:T], ACC[2][:, 0:T])
                for cc in range(2):
                    tt(OT[:, cc, 0:T], ACC[cc][:, 0:T], rinv[:, 0:T], ALU.mult)
                py = PS.next()
                for cc in range(2):
                    o = 2048 + (cc * 8 + h) * 128
                    mm(py[:, 0:T], ukv[:, o:o + 128], OT[:, cc, 0:T], cc == 0, cc == 1)
                cp(mix[:, 8 + h, 0:T], py[:, 0:T], e="act")
                yield


        if kind == "p":
            n_gdn = (T // 128) * (13 + K)
            ggen = gdn128_gen()
        else:
            n_gdn = (T // C) * (14 + K)
            ggen = gdn_gen()
        n_mla = 8 * (5 + 3 * (hist_kt + (T // 128 if kind == "p" else 1)))
        run_interleaved([(ggen, PS_A, 1), (mla_gen(), PS_B, max(1, int(round(n_mla / n_gdn))))])
        fence(G2_bufs, G1_bufs)

        def evac_res(tag, m, ps):
            tt(xT[:, m, 0:T], xT[:, m, 0:T], ps[:, 0:T], ALU.add)

        for g in range(8):
            slot = load_w(w_out_d[l, g], ("out", l, g))
            proj(slot, 16, [(0, 128, "o", 2 * g), (128, 128, "o", 2 * g + 1)], lambda k: mix[:, k, 0:T], T, evac_res)

        rmsnorm_x(16, T, actb)

        def evac_up(tag, f, ps):
            r = R32.next()
            act(r[:, 0:T], ps[:, 0:T], AF.Relu)
            tt(hid[:, f, 0:T], r[:, 0:T], r[:, 0:T], ALU.mult)

        fence(M_bufs, [hid])
        for fc in range(4):
            for g in range(8):
                slot = load_w(w_up_d[l, fc * 8 + g], ("up", l, fc * 8 + g))
                proj(slot, 16, [(0, 128, "f", 2 * g), (128, 128, "f", 2 * g + 1)], lambda k: actb[:, k, 0:T], T, evac_up)
            for mp in range(8):
                slot = load_w(w_down_d[l, fc, mp], ("down", l, fc, mp))
                w4 = slot.ap.rearrange("p (a k n) -> p a k n", a=2, k=16)
                for mi in range(2):
                    ps = PS.next()
                    for k in range(16):
                        mm(ps[:, 0:T], Tl(w4[:, mi, k, :], slot.b), hid[:, k, 0:T], k == 0, k == 15)
                    evac_res("d", mp * 2 + mi, ps)
        fence([hid], M_bufs)

        if l == 0:
            dcols = slice(tok0, tok0 + T) if kind == "p" else slice(SEQ, SEQ + TS)
            dma("pool", xres_d.rearrange("(c p) t -> p c t", p=128)[:, :, dcols], xT[:, :, 0:T], writes=[bx])
        else:
            rmsnorm_x(94, T, xT)
            ydst = yT_d if kind == "p" else ysT_d
            dma("pool", ydst.rearrange("(c p) t -> p c t", p=128)[:, :, ocols], xT[:, :, 0:T])

    for l in range(2):
        if l == 1:
            convert_w(weight_keys(1))
        dma("pool", cvec, cvec_d[l])
        dma("pool", wsf, wsT_d[l].rearrange("p (g i) -> p g i", g=4))
        dma("pool", bsf, bs_d[l])
        tt(wsm, wsf, Tl(c_gm.ap.unsqueeze(1).to_broadcast([128, 4, 128]), c_gm.b), ALU.mult)
        cp(bsb, bsf)
        act(nexpa, cvec[0:4, 110:111], AF.Exp)
        ts(nexpa, nexpa, -1.0, None, ALU.mult)
        memset(Sst, 0.0)
        memset(Sbf, 0.0)
        memset(ctail_p, 0.0)
        for ti in range(NT_RUN):
            process_tile(l, "p", ti)
        dma("pool", gs_o[l], Sst)
        dma("pool", conv_o[l], ctail_p)
        dma("pool", ckvT[:, :, 0:PAST], ckvpT_d[l].rearrange("(c p) k -> p c k", p=128))
        dma("pool", Vc[:, 0:16, :], ckvp_d[l].rearrange("(kt p) c -> p kt c", p=128))
        dma("pool", krT[:, 0:PAST], krpT_d[l])
        dma("pool", Sst, s0_d[l])
        cp(Sbf, Sst, e="act")
        dma("pool", ctail_s, convp_d[l])
        process_tile(l, "s", 0)
        dma("pool", gss_o[l], Sst)
        dma("pool", convs_o[l], ctail_s)
    S.finish()
    print("[kernel] instructions:", S.n_ins, "sbuf bytes left:", nc.sbuf_bytes_remaining, flush=True)
    return nc, S


OFF_AU, OFF_AV, OFF_BQKV, OFF_BZ, OFF_BBETA, OFF_BALPHA, OFF_CQ, OFF_CKV, OFF_CKR, IN_COLS = (
    0, 512, 1024, 2560, 3072, 3076, 3080, 3592, 3848, 3912)

_PROG = None


def _tile_w(w, nk):
    n = w.shape[1]
    return np.ascontiguousarray(w.reshape(nk, 128, n).transpose(1, 0, 2).reshape(128, nk * n))


def _prep_weights(inp):
    f = np.float32
    sw = np.concatenate([np.arange(32, 64), np.arange(0, 32)])
    w_in = np.asarray(inp["w_in"], f)
    cols = np.concatenate([
        np.arange(0, 3072),
        np.arange(OFF_CQ, OFF_CKV), np.arange(OFF_CKV, OFF_CKR),
        np.arange(OFF_CKR, IN_COLS), OFF_CKR + sw,
        np.arange(OFF_BBETA, OFF_BALPHA), np.arange(OFF_BALPHA, OFF_CQ)])
    w_in_t = np.zeros((2, 16, 128, 4096), f)
    for l in range(2):
        wp = np.zeros((D, 4096), f)
        wp[:, :cols.size] = w_in[l][:, cols]
        for g in range(16):
            w_in_t[l, g] = _tile_w(wp[:, g * 256:(g + 1) * 256], 16)
    w_out = np.asarray(inp["w_out"], f)
    w_out_t = np.stack([np.stack([_tile_w(w_out[l][:, g * 256:(g + 1) * 256], 16) for g in range(8)]) for l in range(2)])
    w_up = np.asarray(inp["w_up"], f)
    w_up_t = np.stack([np.stack([_tile_w(w_up[l][:, g * 256:(g + 1) * 256], 16) for g in range(32)]) for l in range(2)])
    w_down = np.asarray(inp["w_down"], f)
    w_down_t = np.zeros((2, 4, 8, 128, 4096), f)
    for l in range(2):
        for fc in range(4):
            for mp in range(8):
                blk = w_down[l][fc * 2048:(fc + 1) * 2048, mp * 256:(mp + 1) * 256]
                t = blk.reshape(16, 128, 2, 128).transpose(1, 2, 0, 3)
                w_down_t[l, fc, mp] = t.reshape(128, 4096)
    w_uq = np.asarray(inp["c_w_uq"], f)
    w_uq_t = np.zeros((2, 2, 128, 4096), f)
    for l in range(2):
        per_head = np.concatenate([w_uq[l][:, :, 0:128], w_uq[l][:, :, 128:192], w_uq[l][:, :, 128 + sw]], axis=2)
        for hg in range(2):
            blk = per_head[:, hg * 4:(hg + 1) * 4, :].reshape(512, 1024)
            w_uq_t[l, hg] = _tile_w(blk, 4)
    w_uk = np.asarray(inp["c_w_uk"], f)
    w_uv = np.asarray(inp["c_w_uv"], f)
    w_ukv_t = np.zeros((2, 128, 4096), f)
    for l in range(2):
        w_ukv_t[l, :, 0:2048] = w_uk[l].transpose(2, 1, 0).reshape(128, 2048)
        w_ukv_t[l, :, 2048:4096] = w_uv[l].reshape(2, 128, 8, 128).transpose(1, 0, 2, 3).reshape(128, 2048)
    cvec = np.zeros((2, 128, 128), f)

    def pc(v):
        return np.asarray(v, f).reshape(-1, 128).T

    for l in range(2):
        cvec[l, :, 0:16] = pc(inp["attn_norm_w"][l])
        cvec[l, :, 16:32] = pc(inp["mlp_norm_w"][l])
        cvec[l, :, 32:36] = pc(inp["a_ln_w"][l])
        cvec[l, :, 36:40] = pc(inp["a_ln_b"][l])
        cw = np.asarray(inp["b_conv_w"][l], f)
        cvec[l, :, 40:88] = cw.reshape(4, 12, 128).transpose(2, 1, 0).reshape(128, 48)
        cvec[l, :, 88:92] = pc(inp["c_q_norm_w"][l])
        cvec[l, :, 92:94] = pc(inp["c_kv_norm_w"][l])
        cvec[l, :, 94:110] = pc(inp["final_norm_w"])
        cvec[l, 0:4, 110] = np.asarray(inp["b_a_log"][l], f)
        cvec[l, 0:4, 111] = np.asarray(inp["b_dt_bias"][l], f)
        cvec[l, :, 112] = np.asarray(inp["b_norm_w"][l], f)
    a_w_s = np.asarray(inp["a_w_s"], f)
    wsT = np.ascontiguousarray(a_w_s.transpose(0, 3, 1, 2).reshape(2, 128, 512))
    bs = np.ascontiguousarray(np.asarray(inp["a_b_s"], f).reshape(2, 1, 512))
    pos = np.concatenate([np.arange(SEQ), PAST + np.arange(TS)]).astype(f)
    inv = (np.float32(10000.0) ** (-np.arange(32, dtype=f) / np.float32(32))).astype(f)
    ang = (pos[None, :] * inv[:, None]).astype(f)
    cos, sin = np.cos(ang).astype(f), np.sin(ang).astype(f)
    cos2 = np.ascontiguousarray(np.concatenate([cos, cos], 0))
    sin2 = np.ascontiguousarray(np.concatenate([-sin, sin], 0))
    idx = np.arange(128)
    same = (idx[:, None] // 64) == (idx[None, :] // 64)
    cst = np.concatenate([
        np.eye(128, dtype=f),
        ((idx[:, None] <= idx[None, :]) & same).astype(f),
        ((idx[:, None] < idx[None, :]) & same).astype(f),
        ((idx[None, :] // 64) >= (idx[:, None] // 64)).astype(f),
        same.astype(f)], axis=1)
    return dict(w_in_t=w_in_t, w_out_t=w_out_t, w_up_t=w_up_t, w_down_t=w_down_t, w_uq_t=w_uq_t,
                w_ukv_t=w_ukv_t, cvec=cvec, wsT=wsT, bs=bs, cos2=cos2, sin2=sin2,
                cst=np.ascontiguousarray(cst))


def _prep_inputs(inp):
    f = np.float32
    shared = _prep_weights(inp)
    xp = np.asarray(inp["x_prompt"], f)
    xs = np.asarray(inp["x_sample"], f)
    ckv = np.asarray(inp["cache_mla_ckv"], f)
    kr = np.asarray(inp["cache_mla_krope"], f)
    sg = np.asarray(inp["state_gdn"], f)
    sc = np.asarray(inp["state_gdn_conv"], f)
    maps = []
    for c in range(8):
        m = dict(shared)
        m["xT"] = np.ascontiguousarray(xp[c].T) if c < 4 else np.zeros((D, SEQ), f)
        m["xsT"] = np.ascontiguousarray(xs[c].T)
        m["ckvpT"] = np.ascontiguousarray(ckv[:, c].transpose(0, 2, 1))
        m["ckvp"] = np.ascontiguousarray(ckv[:, c])
        m["krpT"] = np.ascontiguousarray(kr[:, c].transpose(0, 2, 1))
        m["s0"] = np.ascontiguousarray(sg[:, c].transpose(0, 2, 1, 3).reshape(2, 128, 512))
        m["convp"] = np.ascontiguousarray(sc[:, c].reshape(2, 3, 12, 128).transpose(0, 3, 2, 1).reshape(2, 128, 36))
        maps.append(m)
    return maps


def _conv_back(a):
    return a.reshape(2, 128, 12, 3).transpose(0, 3, 2, 1).reshape(2, 3, 1536)


def _state_back(a):
    return a.reshape(2, 128, 4, 128).transpose(0, 2, 1, 3)


def kernel(**inputs):
    global _PROG
    if _PROG is None:
        _PROG = build_program()[0]
    nc = _PROG
    in_maps = _prep_inputs(inputs)
    res = run_bass_kernel_spmd(nc, in_maps, core_ids=list(range(8)))
    r = res.results
    f = np.float32
    y_prompt = np.stack([r[c]["yT"].T for c in range(4)]).astype(f)
    y_sample = np.stack([r[c]["ysT"].T for c in range(8)]).astype(f)
    ckv_p = np.stack([r[c]["ckv_o"].transpose(0, 2, 1) for c in range(4)], axis=1).astype(f)
    kr_p = np.stack([r[c]["kr_o"].transpose(0, 2, 1) for c in range(4)], axis=1).astype(f)
    gs_p = np.stack([_state_back(r[c]["gs_o"]) for c in range(4)], axis=1).astype(f)
    conv_p = np.stack([_conv_back(r[c]["conv_o"]) for c in range(4)], axis=1).astype(f)
    ckv_s = np.stack([r[c]["ckvs_o"].transpose(0, 2, 1) for c in range(8)], axis=1).astype(f)
    kr_s = np.stack([r[c]["krs_o"].transpose(0, 2, 1) for c in range(8)], axis=1).astype(f)
    gs_s = np.stack([_state_back(r[c]["gss_o"]) for c in range(8)], axis=1).astype(f)
    conv_s = np.stack([_conv_back(r[c]["convs_o"]) for c in range(8)], axis=1).astype(f)
    vn_s = np.stack([r[c]["vns_o"].transpose(0, 2, 1) for c in range(8)], axis=1).astype(f)
    return (np.ascontiguousarray(y_prompt), np.ascontiguousarray(y_sample),
            np.ascontiguousarray(ckv_p), np.ascontiguousarray(kr_p), np.ascontiguousarray(gs_p),
            np.ascontiguousarray(conv_p), np.ascontiguousarray(ckv_s), np.ascontiguousarray(kr_s),
            np.ascontiguousarray(gs_s), np.ascontiguousarray(conv_s), np.ascontiguousarray(vn_s))
```

```python
import os
import numpy as np
import concourse.bass as bass
import concourse.mybir as mybir
from concourse.bass_utils import run_bass_kernel_spmd

F32 = mybir.dt.float32
BF16 = mybir.dt.bfloat16
AF = mybir.ActivationFunctionType
ALU = mybir.AluOpType
AX = mybir.AxisListType

D = 2048
SEQ = 4096
TP = 512
NTILE = SEQ // TP
TS = 32
PAST = 2048
NTOK = SEQ + TS
C_SCALE = 192 ** -0.5
EPS = 1e-6
SEM_LIMIT = 30000
NT_RUN = int(os.environ.get("MK_NT_RUN", NTILE))


class Buf:
    __slots__ = ("name", "w", "r", "excl")

    def __init__(self, name="", excl=False):
        self.name = name
        self.w = None
        self.r = {}
        self.excl = excl


class Tl:
    __slots__ = ("ap", "b")

    def __init__(self, ap, b):
        self.ap = ap
        self.b = b

    def __getitem__(self, k):
        return Tl(self.ap[k], self.b)


class Sched:
    def __init__(self, nc, n_dma_sems=10):
        self.nc = nc
        self.eng = {"pe": nc.tensor, "dve": nc.vector, "act": nc.scalar,
                    "pool": nc.gpsimd, "sp": nc.sync}
        self.sem = {}
        self.cnt = {}
        self.nsem = 0
        for e in self.eng:
            self._new_sem(e)
        self.waited = {}
        self.pending = {e: False for e in self.eng}
        self.dma_pool = {}
        for q in ("sp", "pool"):
            self.dma_pool[q] = [[self._alloc(f"d{q}{i}"), 0] for i in range(n_dma_sems)]
        self.dma_rr = {"sp": 0, "pool": 0}
        self.n_ins = 0

    def _alloc(self, name):
        self.nsem += 1
        return self.nc.alloc_semaphore(name=f"{name}_{self.nsem}")

    def _new_sem(self, e):
        self.sem[e] = self._alloc(f"s{e}")
        self.cnt[e] = 0

    def _wait(self, e, tok):
        sem, val = tok
        key = (e, id(sem))
        if self.waited.get(key, 0) >= val:
            return
        self.eng[e].wait_ge(sem, val)
        self.waited[key] = val

    def _deps(self, e, reads, writes):
        toks = []
        for b in reads:
            if b.w is not None:
                toks.append(b.w)
            if b.excl:
                toks.extend(t for t in b.r.values() if t[0] is not self.sem[e])
        for b in writes:
            if b.w is not None:
                toks.append(b.w)
            toks.extend(b.r.values())
        for tok in toks:
            if e == "pe" and tok[0] is self.sem["pe"]:
                continue
            self._wait(e, tok)

    def _mark(self, tok, reads, writes):
        for b in reads:
            cur = b.r.get(id(tok[0]))
            if cur is None or cur[1] < tok[1]:
                b.r[id(tok[0])] = tok
        for b in writes:
            b.w = tok
            b.r = {}

    def op(self, e, fn, reads=(), writes=(), sig=True):
        self._deps(e, reads, writes)
        if sig and self.cnt[e] >= SEM_LIMIT and not self.pending[e]:
            self._new_sem(e)
        ins = fn()
        self.n_ins += 1
        if sig:
            ins.then_inc(self.sem[e], 1)
            self.cnt[e] += 1
            tok = (self.sem[e], self.cnt[e])
            self.pending[e] = False
        else:
            tok = (self.sem[e], self.cnt[e] + 1)
            self.pending[e] = True
        self._mark(tok, reads, writes)
        return ins

    def dma(self, q, out, in_, reads=(), writes=()):
        self._deps(q, reads, writes)
        pool = self.dma_pool[q]
        i = self.dma_rr[q]
        self.dma_rr[q] = (i + 1) % len(pool)
        ent = pool[i]
        if ent[1] > 0:
            self._wait(q, (ent[0], ent[1]))
        if ent[1] >= SEM_LIMIT:
            ent[0] = self._alloc(f"d{q}")
            ent[1] = 0
        ins = self.eng[q].dma_start(out=out, in_=in_)
        ins.then_inc(ent[0], 16)
        ent[1] += 16
        self.n_ins += 1
        tok = (ent[0], ent[1])
        self._mark(tok, reads, writes)
        return ins

    def finish(self):
        for q, pool in self.dma_pool.items():
            for sem, val in pool:
                if val > 0:
                    self._wait("sp", (sem, val))
        for e in ("pe", "dve", "act", "pool"):
            if self.cnt[e] > 0:
                self._wait("sp", (self.sem[e], self.cnt[e]))


class Ring:
    def __init__(self, tiles):
        self.tiles = tiles
        self.i = 0

    def next(self):
        t = self.tiles[self.i]
        self.i = (self.i + 1) % len(self.tiles)
        return t


def build_program():
    nc = bass.Bass("TRN2", target_bir_lowering=False)
    S = Sched(nc)

    def din(name, shape):
        return nc.dram_tensor(name, list(shape), F32, kind="ExternalInput").ap()

    def dout(name, shape):
        return nc.dram_tensor(name, list(shape), F32, kind="ExternalOutput").ap()

    xT_d = din("xT", [D, SEQ])
    xsT_d = din("xsT", [D, TS])
    ckvpT_d = din("ckvpT", [2, 256, PAST])
    ckvp_d = din("ckvp", [2, PAST, 256])
    krpT_d = din("krpT", [2, 64, PAST])
    s0_d = din("s0", [2, 128, 512])
    convp_d = din("convp", [2, 128, 36])
    w_in_d = din("w_in_t", [2, 16, 128, 4096])
    w_out_d = din("w_out_t", [2, 8, 128, 4096])
    w_up_d = din("w_up_t", [2, 32, 128, 4096])
    w_down_d = din("w_down_t", [2, 4, 8, 128, 4096])
    w_uq_d = din("w_uq_t", [2, 2, 128, 4096])
    w_ukv_d = din("w_ukv_t", [2, 128, 4096])
    cvec_d = din("cvec", [2, 128, 128])
    wsT_d = din("wsT", [2, 128, 512])
    bs_d = din("bs", [2, 1, 512])
    cos_d = din("cos2", [64, NTOK])
    sin_d = din("sin2", [64, NTOK])
    cst_d = din("cst", [128, 640])

    yT_d = dout("yT", [D, SEQ])
    ysT_d = dout("ysT", [D, TS])
    ckv_o = dout("ckv_o", [2, 256, SEQ])
    kr_o = dout("kr_o", [2, 64, SEQ])
    gs_o = dout("gs_o", [2, 128, 512])
    conv_o = dout("conv_o", [2, 128, 36])
    ckvs_o = dout("ckvs_o", [2, 256, TS])
    krs_o = dout("krs_o", [2, 64, TS])
    gss_o = dout("gss_o", [2, 128, 512])
    convs_o = dout("convs_o", [2, 128, 36])
    vns_o = dout("vns_o", [2, 512, TS])
    xres_d = nc.dram_tensor("xres", [D, NTOK], F32, kind="Internal").ap()
    Bxres = [Buf(f"xres{i}") for i in range(NTILE + 1)]
    NWG = 2 * (16 + 8 + 32 + 32 + 3)
    wscr_d = nc.dram_tensor("wscr", [NWG, 128, 4096], BF16, kind="Internal").ap()
    wscr_idx = {}

    def sb(name, shape, dt):
        return Tl(nc.alloc_sbuf_tensor("sb_" + name, list(shape), dt).ap(), Buf(name))

    xT = sb("xT", [128, 16, TP], F32)
    actb = sb("actb", [128, 16, TP], BF16)
    mix = Tl(actb.ap, actb.b)
    arenaM = nc.alloc_sbuf_tensor("sb_arenaM", [128, 4 * 4 * TP // 2], F32).ap()
    arenaMb = arenaM.bitcast(BF16)
    arenaG = nc.alloc_sbuf_tensor("sb_arenaG", [128, 4352], F32).ap()
    arenaGb = arenaG.bitcast(BF16)

    def carve(ap2d, a, name):
        return Tl(ap2d.rearrange("p (a t) -> p a t", a=a), Buf(name))

    hid = carve(arenaMb[:, 0:16 * TP], 16, "hid")
    ckvT = sb("ckvT", [128, 2, SEQ], BF16)
    krT = sb("krT", [64, SEQ], BF16)
    Vc = sb("Vc", [128, 32, 256], BF16)
    WS = Ring([sb(f"wslot{i}", [128, 4096], BF16) for i in range(3)])
    R32 = Ring([sb(f"r32_{i}", [128, 512], F32) for i in range(6)])
    R16 = Ring([sb(f"r16_{i}", [128, 512], BF16) for i in range(6)])
    CIN = Ring([sb(f"cin{i}", [128, TP + 3], F32) for i in range(2)])
    uT = carve(arenaGb[:, 0:2048], 4, "uT")
    v32 = carve(arenaG[:, 1024:3072], 4, "v32")
    vnb = carve(arenaGb[:, 6144:8192], 4, "vnb")
    qh = carve(arenaMb[:, 0:2048], 4, "qh")
    kh = carve(arenaMb[:, 2048:4096], 4, "kh")
    vT = carve(arenaMb[:, 4096:6144], 4, "vT")
    sz = carve(arenaMb[:, 6144:8192], 4, "sz")
    cqn = sb("cqn", [128, 4, TP], BF16)
    ckv32 = sb("ckv32", [128, 2, TP], F32)
    betaT = sb("betaT", [4, TP], F32)
    gT = sb("gT", [4, TP], F32)
    cosT = sb("cosT", [64, TP], F32)
    sinT = sb("sinT", [64, TP], F32)
    cst = sb("cst", [128, 640], F32)
    cvec = sb("cvec", [128, 128], F32)
    wsf = sb("wsf", [128, 4, 128], F32)
    wsm = sb("wsm", [128, 4, 128], BF16)
    bsf = sb("bsf", [1, 512], F32)
    bsb = sb("bsb", [1, 512], BF16)
    identb = sb("identb", [128, 128], BF16)
    onesb = sb("onesb", [128, 128], BF16)
    zerosb = sb("zerosb", [128, 128], BF16)
    onesf = sb("onesf", [128, 128], F32)
    incl4 = sb("incl4", [128, 4, 128], BF16)
    strict4 = sb("strict4", [128, 4, 128], BF16)
    ident4 = sb("ident4", [128, 4, 128], BF16)
    elast8 = sb("elast8", [128, 8], F32)
    gab = sb("gab", [128, 8], F32)
    vnew2 = sb("vnew2", [128, 512], BF16)
    incl32 = sb("incl32", [32, 4, 32], BF16)
    strict32 = sb("strict32", [32, 4, 32], BF16)
    ident32 = sb("ident32", [32, 4, 32], BF16)
    epsc = sb("epsc", [128, 1], F32)
    eps5 = sb("eps5", [128, 1], F32)
    onec = sb("onec", [128, 1], F32)
    nexpa = sb("nexpa", [4, 1], F32)
    Sst = sb("Sst", [128, 512], F32)
    Sbf = sb("Sbf", [128, 512], BF16)
    ctail_p = sb("ctail_p", [128, 36], F32)
    ctail_s = sb("ctail_s", [128, 36], F32)
    bg = sb("bg", [128, 8], F32)
    gcc = sb("gcc", [128, 4], F32)
    egc = sb("egc", [128, 4], F32)
    elast = sb("elast", [128, 4], F32)
    ssn = sb("ssn", [128, 4], F32)
    Dm = Tl(arenaG[:, 0:512], Buf("Dm"))
    og = Tl(arenaG[:, 512:1024], Buf("og"))
    gb = {n: Tl(arenaGb[:, 2048 + i * 512:2048 + (i + 1) * 512], Buf("g_" + n)) for i, n in enumerate(
          ["qkT", "Xa", "Xb", "Ya", "Yb", "R", "RT", "vtok", "kg", "kdec", "nXk", "vnew", "on"])}
    G1_bufs = [uT, v32, vnb]
    G2_bufs = [Dm, og] + list(gb.values())
    M_bufs = [qh, kh, vT, sz]

    def fence(srcs, dsts):
        toks = {}
        for s_ in srcs:
            for tok in ([s_.b.w] if s_.b.w is not None else []) + list(s_.b.r.values()):
                k = id(tok[0])
                if k not in toks or toks[k][1] < tok[1]:
                    toks[k] = tok
        for d in dsts:
            for k, tok in toks.items():
                cur = d.b.r.get(k)
                if cur is None or cur[1] < tok[1]:
                    d.b.r[k] = tok
    qn = sb("qn", [128, TP], BF16)
    qr = sb("qr", [64, TP], BF16)
    qlat = sb("qlat", [128, 2, TP], BF16)
    OT = sb("OT", [128, 2, TP], BF16)

    PSall = [Tl(nc.alloc_psum_tensor(f"psb{i}", [128, 512], F32).ap(), Buf(f"ps{i}", excl=True)) for i in range(8)]
    class RingSel:
        def __init__(self, ring):
            self.ring = ring

        def next(self):
            return self.ring.next()

    PS_ALL, PS_A, PS_B = Ring(PSall[0:5]), Ring(PSall[0:3]), Ring(PSall[3:5])
    PS = RingSel(PS_ALL)

    def run_interleaved(pairs):
        live = list(pairs)
        while live:
            for item in list(live):
                PS.ring = item[1]
                try:
                    for _ in range(item[2]):
                        next(item[0])
                except StopIteration:
                    live.remove(item)
        PS.ring = PS_ALL
    ACC = PSall[5:8]

    def bfv(t):
        return Tl(t.ap.bitcast(BF16), t.b)

    def v3(t, P, A, Bn):
        return Tl(t.ap[0:P, 0:A * Bn].rearrange("p (a b) -> p a b", a=A), t.b)

    def bc(t, n):
        return Tl(t.ap.unsqueeze(2).to_broadcast([t.ap.shape[0], t.ap.shape[1], n]), t.b)

    def _rb(xs):
        return [x.b for x in xs if isinstance(x, Tl)]

    def _a(x):
        return x.ap if isinstance(x, Tl) else x

    def mm(out, lhsT, rhs, start, stop, sig=None):
        S.op("pe", lambda: nc.tensor.matmul(out.ap, lhsT=lhsT.ap, rhs=rhs.ap, start=start, stop=stop),
             reads=[lhsT.b, rhs.b], writes=[out.b], sig=(stop if sig is None else sig))

    def tr(out, in_, ident, sig=True):
        S.op("pe", lambda: nc.tensor.transpose(out.ap, in_.ap, ident.ap),
             reads=[in_.b, ident.b], writes=[out.b], sig=sig)

    def act(out, in_, func, bias=None, scale=1.0):
        kw = {}
        if bias is not None:
            kw["bias"] = _a(bias)
        S.op("act", lambda: nc.scalar.activation(out=out.ap, in_=in_.ap, func=func, scale=_a(scale), **kw),
             reads=[in_.b] + _rb([bias, scale]), writes=[out.b])

    def tt(out, in0, in1, op, e="dve"):
        S.op(e, lambda: S.eng[e].tensor_tensor(out=out.ap, in0=in0.ap, in1=in1.ap, op=op),
             reads=[in0.b, in1.b], writes=[out.b])

    def ts(out, in0, s1, s2, op0, op1=None, e="dve"):
        kw = {} if op1 is None else {"op1": op1}
        S.op(e, lambda: S.eng[e].tensor_scalar(out=out.ap, in0=in0.ap, scalar1=_a(s1), scalar2=_a(s2), op0=op0, **kw),
             reads=[in0.b] + _rb([s1, s2]), writes=[out.b])

    def stt(out, in0, scalar, in1, op0, op1, e="dve"):
        S.op(e, lambda: S.eng[e].scalar_tensor_tensor(out=out.ap, in0=in0.ap, scalar=_a(scalar), in1=in1.ap, op0=op0, op1=op1),
             reads=[in0.b, in1.b] + _rb([scalar]), writes=[out.b])

    def cp(out, in_, e="dve"):
        if e == "act":
            act(out, in_, AF.Copy)
        else:
            S.op(e, lambda: S.eng[e].tensor_copy(out=out.ap, in_=in_.ap), reads=[in_.b], writes=[out.b])

    def recip(out, in_):
        S.op("dve", lambda: nc.vector.reciprocal(out=out.ap, in_=in_.ap), reads=[in_.b], writes=[out.b])

    def memset(out, val, e="dve"):
        S.op(e, lambda: S.eng[e].memset(out.ap, val), writes=[out.b])

    def dma(q, out, in_, reads=(), writes=()):
        S.dma(q, _a(out), _a(in_), reads=list(reads) + _rb([in_]), writes=list(writes) + _rb([out]))

    dma("pool", cst, cst_d)
    c_id, c_incl, c_strict, c_gm = (cst[:, i * 128:(i + 1) * 128] for i in range(4))
    cp(identb, c_id)
    memset(onesb, 1.0)
    memset(zerosb, 0.0)
    memset(onesf, 1.0)
    memset(epsc, EPS)
    memset(eps5, 1e-5)
    memset(onec, 1.0)
    for dst, src in ((incl32, c_incl), (strict32, c_strict), (ident32, c_id)):
        cp(dst, Tl(src.ap[0:32, 0:32].unsqueeze(1).to_broadcast([32, 4, 32]), src.b))
    for dst, src in ((incl4, c_incl), (strict4, c_strict), (ident4, c_id)):
        cp(dst, Tl(src.ap.unsqueeze(1).to_broadcast([128, 4, 128]), src.b))

    def weight_keys(l):
        ks = [(("in", l, g), w_in_d[l, g]) for g in range(16)]
        ks += [(("uq", l, 0), w_uq_d[l, 0]), (("uq", l, 1), w_uq_d[l, 1]), (("ukv", l), w_ukv_d[l])]
        ks += [(("out", l, g), w_out_d[l, g]) for g in range(8)]
        for fc in range(4):
            ks += [(("up", l, fc * 8 + g), w_up_d[l, fc * 8 + g]) for g in range(8)]
            ks += [(("down", l, fc, mp), w_down_d[l, fc, mp]) for mp in range(8)]
        return ks

    def convert_w(items):
        for key, src in items:
            if key in wscr_idx:
                continue
            wscr_idx[key] = (len(wscr_idx), Buf("wscr"))
            i, b_ = wscr_idx[key]
            dma("pool", wscr_d[i], src, writes=[b_])

    def load_w(src, key):
        if key not in wscr_idx:
            convert_w([(key, src)])
        i, b_ = wscr_idx[key]
        slot = WS.next()
        dma("sp", slot, wscr_d[i], reads=[b_])
        return slot

    def rmsnorm_x(wcol, T, out3, f32_out=False):
        pss = PS.next()
        for c in range(16):
            sq = R16.next()
            act(sq[:, 0:T], xT[:, c, 0:T], AF.Square)
            mm(pss[:, 0:T], onesb, sq[:, 0:T], c == 0, c == 15, sig=True)
        rs = R32.next()
        act(rs[:, 0:T], pss[:, 0:T], AF.Sqrt, bias=epsc, scale=1.0 / D)
        recip(rs[:, 0:T], rs[:, 0:T])
        for c in range(16):
            stt(out3[:, c, 0:T], xT[:, c, 0:T], cvec[:, wcol + c:wcol + c + 1], rs[:, 0:T], ALU.mult, ALU.mult)

    deferred = []

    def flush_deferred(keep=0):
        for ent in deferred:
            ent[1] += 1
        while deferred and deferred[0][1] > keep:
            deferred.pop(0)[0]()

    def proj(slot, nk, mtiles, rhs_of_k, T, evac, acc_ps=None, k0=0, ktot=None):
        w = slot.ap.rearrange("p (k n) -> p k n", k=nk)
        ktot = nk if ktot is None else ktot
        for (co, M, tag, idx) in mtiles:
            ps = PS.next() if acc_ps is None else acc_ps
            for k in range(nk):
                mm(ps[0:M, 0:T], Tl(w[:, k, co:co + M], slot.b), rhs_of_k(k0 + k),
                   (k0 + k) == 0, (k0 + k) == ktot - 1)
            flush_deferred(keep=2)
            if k0 + nk == ktot:
                evac(tag, idx, ps)

    WIN_GROUPS = []
    for tg in ("u", "v", "q", "k", "vv", "z", "cq"):
        WIN_GROUPS.append([(0, 128, tg, 0), (128, 128, tg, 1)])
        WIN_GROUPS.append([(0, 128, tg, 2), (128, 128, tg, 3)])
    WIN_GROUPS.append([(0, 128, "ckv", 0), (128, 128, "ckv", 1)])
    WIN_GROUPS.append([(0, 64, "kr", 0), (64, 64, "krsw", 0), (128, 4, "beta", 0), (132, 4, "alpha", 0)])

    def process_tile(l, kind, ti):
        if kind == "p":
            T, C, K, tok0, key0, hist_kt, pcol = TP, 64, 5, ti * TP, ti * TP, (ti * TP) // 128, ti * TP
            ctail, m_incl, m_strict, m_ident = ctail_p, None, None, None
            ckv_out, kr_out = ckv_o[l], kr_o[l]
            xsrc = xT_d if l == 0 else xres_d
            xcols = slice(tok0, tok0 + T)
            ocols = slice(tok0, tok0 + T)
            bx = Bxres[ti]
        else:
            T, C, K, tok0, key0, hist_kt, pcol = TS, 32, 4, 0, PAST, PAST // 128, SEQ
            ctail, m_incl, m_strict, m_ident = ctail_s, incl32, strict32, ident32
            ckv_out, kr_out = ckvs_o[l], krs_o[l]
            xsrc = xsT_d if l == 0 else xres_d
            xcols = slice(0, TS) if l == 0 else slice(SEQ, SEQ + TS)
            ocols = slice(0, TS)
            bx = Bxres[NTILE]
        nblk = (T + 127) // 128

        dma("pool", xT[:, :, 0:T], xsrc.rearrange("(c p) t -> p c t", p=128)[:, :, xcols],
            reads=[bx] if l == 1 else [])
        dma("pool", cosT[:, 0:T], cos_d[:, pcol:pcol + T])
        dma("pool", sinT[:, 0:T], sin_d[:, pcol:pcol + T])

        if kind == "p" and l == 0:
            if ti == 0:
                convert_w(weight_keys(0))
            else:
                k1 = weight_keys(1)
                per = (len(k1) + NT_RUN - 2) // max(1, NT_RUN - 1)
                convert_w(k1[(ti - 1) * per:ti * per])
        rmsnorm_x(0, T, actb)

        st8 = {}

        def evac_in(tag, m, ps):
            if tag == "u":
                act(uT[:, m, 0:T], ps[:, 0:T], AF.Gelu)
            elif tag == "v":
                act(v32[:, m, 0:T], ps[:, 0:T], AF.Gelu)
            elif tag in ("q", "k", "vv"):
                ch = {"q": 0, "k": 4, "vv": 8}[tag] + m
                cin = CIN.next()
                cp(cin[:, 0:3], ctail[:, ch * 3:ch * 3 + 3])
                act(cin[:, 3:3 + T], ps[:, 0:T], AF.Copy)
                acc = R32.next()
                cw = 40 + ch * 4
                ts(acc[:, 0:T], cin[:, 0:T], cvec[:, cw:cw + 1], None, ALU.mult)
                for i in range(1, 4):
                    stt(acc[:, 0:T], cin[:, i:i + T], cvec[:, cw + i:cw + i + 1], acc[:, 0:T], ALU.mult, ALU.add)
                cp(ctail[:, ch * 3:ch * 3 + 3], cin[:, T:T + 3])
                if tag == "vv":
                    act(vT[:, m, 0:T], acc[:, 0:T], AF.Silu)
                else:
                    dst = qh if tag == "q" else kh
                    act(dst[:, m, 0:T], acc[:, 0:T], AF.Silu)
                    sq = R16.next()
                    act(sq[:, 0:T], dst[:, m, 0:T], AF.Square)

                    def fin(dst=dst, sq=sq, tag=tag, m=m):
                        pss = PS.next()
                        mm(pss[:, 0:T], onesb, sq[:, 0:T], True, True)
                        rs = R32.next()
                        act(rs[:, 0:T], pss[:, 0:T], AF.Sqrt, bias=epsc, scale=1.0)
                        recip(rs[:, 0:T], rs[:, 0:T])
                        if tag == "q":
                            stt(dst[:, m, 0:T], dst[:, m, 0:T], float(128 ** -0.5), rs[:, 0:T], ALU.mult, ALU.mult)
                        else:
                            tt(dst[:, m, 0:T], dst[:, m, 0:T], rs[:, 0:T], ALU.mult)

                    deferred.append([fin, 0])
            elif tag == "z":
                act(sz[:, m, 0:T], ps[:, 0:T], AF.Silu)
            elif tag == "cq":
                act(cqn[:, m, 0:T], ps[:, 0:T], AF.Copy)
                if m == 3:
                    pss = PS.next()
                    for c in range(4):
                        sq = R16.next()
                        act(sq[:, 0:T], cqn[:, c, 0:T], AF.Square)
                        mm(pss[:, 0:T], onesb, sq[:, 0:T], c == 0, c == 3, sig=True)
                    rs = R32.next()
                    act(rs[:, 0:T], pss[:, 0:T], AF.Sqrt, bias=epsc, scale=1.0 / 512)
                    recip(rs[:, 0:T], rs[:, 0:T])
                    for c in range(4):
                        stt(cqn[:, c, 0:T], cqn[:, c, 0:T], cvec[:, 88 + c:89 + c], rs[:, 0:T], ALU.mult, ALU.mult)
            elif tag == "ckv":
                act(ckv32[:, m, 0:T], ps[:, 0:T], AF.Copy)
                if m == 1:
                    pss = PS.next()
                    for c in range(2):
                        sq = R16.next()
                        act(sq[:, 0:T], ckv32[:, c, 0:T], AF.Square)
                        mm(pss[:, 0:T], onesb, sq[:, 0:T], c == 0, c == 1, sig=True)
                    rs = R32.next()
                    act(rs[:, 0:T], pss[:, 0:T], AF.Sqrt, bias=epsc, scale=1.0 / 256)
                    recip(rs[:, 0:T], rs[:, 0:T])
                    for c in range(2):
                        stt(ckv32[:, c, 0:T], ckv32[:, c, 0:T], cvec[:, 92 + c:93 + c], rs[:, 0:T], ALU.mult, ALU.mult)
                    dma("pool", ckv_out.rearrange("(c p) t -> p c t", p=128)[:, :, ocols], ckv32[:, :, 0:T])
                    cp(ckvT[:, :, key0:key0 + T], ckv32[:, :, 0:T], e="act")
                    def fin_v():
                        for kb in range(nblk):
                            nk = min(128, T - kb * 128)
                            pb = bfv(PS.next())
                            for c in range(2):
                                tr(pb[0:nk, c * 128:(c + 1) * 128],
                                   ckvT[:, c, key0 + kb * 128:key0 + kb * 128 + nk], identb, sig=(c == 1))
                            cp(Vc[0:nk, key0 // 128 + kb, :], pb[0:nk, 0:256])

                    deferred.append([fin_v, 0])
            elif tag == "kr":
                st8["kr"] = ps
            elif tag == "krsw":
                t1 = R32.next()
                t2 = R32.next()
                tt(t1[0:64, 0:T], st8["kr"][0:64, 0:T], cosT[:, 0:T], ALU.mult)
                tt(t2[0:64, 0:T], ps[0:64, 0:T], sinT[:, 0:T], ALU.mult)
                tt(t1[0:64, 0:T], t1[0:64, 0:T], t2[0:64, 0:T], ALU.add)
                dma("pool", kr_out[:, ocols], t1[0:64, 0:T])
                cp(krT[:, key0:key0 + T], t1[0:64, 0:T], e="act")
            elif tag == "beta":
                act(betaT[:, 0:T], ps[0:4, 0:T], AF.Sigmoid)
            elif tag == "alpha":
                t1 = R32.next()
                act(t1[0:4, 0:T], ps[0:4, 0:T], AF.Exp, bias=cvec[0:4, 111:112])
                act(t1[0:4, 0:T], t1[0:4, 0:T], AF.Ln, bias=onec[0:4, :])
                ts(gT[:, 0:T], t1[0:4, 0:T], nexpa[:, 0:1], None, ALU.mult)

        for g in range(16):
            slot = load_w(w_in_d[l, g], ("in", l, g))
            proj(slot, 16, WIN_GROUPS[g], lambda k: actb[:, k, 0:T], T, evac_in)
        flush_deferred()

        psm = PS.next()
        for c in range(4):
            mm(psm[:, 0:T], onesf, v32[:, c, 0:T], c == 0, c == 3)
        for c in range(4):
            stt(v32[:, c, 0:T], psm[:, 0:T], -1.0 / 512, v32[:, c, 0:T], ALU.mult, ALU.add)
        psv = PS.next()
        for c in range(4):
            sq = R16.next()
            act(sq[:, 0:T], v32[:, c, 0:T], AF.Square)
            mm(psv[:, 0:T], onesb, sq[:, 0:T], c == 0, c == 3, sig=True)
        rs = R32.next()
        act(rs[:, 0:T], psv[:, 0:T], AF.Sqrt, bias=eps5, scale=1.0 / 512)
        recip(rs[:, 0:T], rs[:, 0:T])
        for c in range(4):
            tt(v32[:, c, 0:T], v32[:, c, 0:T], rs[:, 0:T], ALU.mult)
            ts(v32[:, c, 0:T], v32[:, c, 0:T], cvec[:, 32 + c:33 + c], cvec[:, 36 + c:37 + c], ALU.mult, ALU.add)
        if kind == "s":
            dma("pool", vns_o[l].rearrange("(c p) t -> p c t", p=128), v32[:, :, 0:T])
        cp(vnb[:, :, 0:T], v32[:, :, 0:T], e="act")
        lc = min(T, 128)
        for g in range(4):
            pb = bfv(PS.next())
            for n in range(nblk):
                tr(pb[0:lc, n * 128:(n + 1) * 128], vnb[:, g, n * 128:n * 128 + lc], identb, sig=(n == nblk - 1))
            vtk = R16.next()
            cp(vtk[0:lc, 0:nblk * 128], pb[0:lc, 0:nblk * 128])
            psf = PS.next()
            for n in range(nblk):
                mm(psf[:, n * 128:n * 128 + lc], vtk[0:lc, n * 128:(n + 1) * 128], wsm[0:lc, g, 0:lc], True, False, sig=False)
                mm(psf[:, n * 128:n * 128 + lc], onesb[0:1, 0:128], bsb[0:1, g * 128:g * 128 + lc], False, True,
                   sig=(n == nblk - 1))
            tt(mix[:, g, 0:T], uT[:, g, 0:T], psf[:, 0:T], ALU.mult)

        G = gb
        fence(G1_bufs, G2_bufs)

        def gdn_gen():
            for c0 in range(0, T, C):
                HC = 4 * C
                p1 = PS.next()
                mm(p1[0:C, 0:4], betaT[0:4, c0:c0 + C], cst[0:4, 0:4], True, True, sig=False)
                mm(p1[0:C, 4:8], gT[0:4, c0:c0 + C], cst[0:4, 0:4], True, True)
                cp(bg[0:C, :], p1[0:C, 0:8])
                yield
                bcol = bg[0:C, 0:4]
                gcol = bg[0:C, 4:8]
                p2 = PS.next()
                mm(p2[0:C, 0:4], c_incl[0:C, 0:C], gcol, True, True, sig=False)
                mm(p2[:, 4:8], onesf[0:C, :], gcol, True, True)
                cp(gcc[0:C, :], p2[0:C, 0:4])
                act(egc[0:C, :], p2[0:C, 0:4], AF.Exp)
                act(elast, p2[:, 4:8], AF.Exp)
                yield
                gtri = R32.next()
                tt(v3(gtri, C, 4, C), m_incl[0:C], bc(gcol, C), ALU.mult)
                p3 = PS.next()
                for h in range(4):
                    mm(p3[0:C, h * C:(h + 1) * C], onesf[0:C, 0:C], gtri[0:C, h * C:(h + 1) * C], True, True, sig=(h == 3))
                Dm3 = v3(Dm, C, 4, C)
                tt(Dm3, v3(p3, C, 4, C), bc(gcc[0:C, :], C), ALU.subtract)
                ts(Dm[0:C, 0:HC], Dm[0:C, 0:HC], 0.0, None, ALU.min)
                act(Dm[0:C, 0:HC], Dm[0:C, 0:HC], AF.Exp)
                tt(Dm3, Dm3, m_incl[0:C], ALU.mult)
                yield
                p4 = PS.next()
                p5 = PS.next()
                for h in range(4):
                    mm(p4[0:C, h * C:(h + 1) * C], kh[:, h, c0:c0 + C], kh[:, h, c0:c0 + C], True, True, sig=False)
                for h in range(4):
                    mm(p5[0:C, h * C:(h + 1) * C], kh[:, h, c0:c0 + C], qh[:, h, c0:c0 + C], True, True, sig=(h == 3))
                tt(v3(G["qkT"], C, 4, C), v3(p5, C, 4, C), Dm3, ALU.mult)
                yield
                dms = R32.next()
                dms3 = v3(dms, C, 4, C)
                tt(dms3, Dm3, m_strict[0:C], ALU.mult)
                tt(dms3, dms3, bc(bcol, C), ALU.mult)
                stt(v3(G["Xa"], C, 4, C), v3(p4, C, 4, C), -1.0, dms3, ALU.mult, ALU.mult)
                yield
                pb = bfv(PS.next())
                for h in range(4):
                    tr(pb[0:C, h * C:(h + 1) * C], G["Xa"][0:C, h * C:(h + 1) * C], identb[0:C, 0:C], sig=(h == 3))
                cp(G["Ya"][0:C, 0:HC], pb[0:C, 0:HC], e="act")
                tt(v3(G["R"], C, 4, C), v3(G["Xa"], C, 4, C), m_ident[0:C], ALU.add)
                yield
                X, Y, Xn, Yn = G["Xa"], G["Ya"], G["Xb"], G["Yb"]
                for sN in range(1, K + 2):
                    need_x = sN < K
                    need_y = sN <= K
                    need_p = sN >= 2
                    if need_x:
                        px = PS.next()
                        for h in range(4):
                            hs = slice(h * C, (h + 1) * C)
                            mm(px[0:C, hs], Y[0:C, hs], X[0:C, hs], True, True, sig=(h == 3))
                    if need_y:
                        py = PS.next()
                        for h in range(4):
                            hs = slice(h * C, (h + 1) * C)
                            mm(py[0:C, hs], X[0:C, hs], Y[0:C, hs], True, True, sig=(h == 3))
                    if need_p:
                        pr = PS.next()
                        for h in range(4):
                            hs = slice(h * C, (h + 1) * C)
                            mm(pr[0:C, hs], Y[0:C, hs], G["R"][0:C, hs], True, True, sig=(h == 3))
                    if need_x:
                        cp(Xn[0:C, 0:HC], px[0:C, 0:HC], e="act")
                    if need_y:
                        cp(Yn[0:C, 0:HC], py[0:C, 0:HC])
                    if need_p:
                        tt(G["R"][0:C, 0:HC], G["R"][0:C, 0:HC], pr[0:C, 0:HC], ALU.add)
                    X, Y, Xn, Yn = Xn, Yn, X, Y
                    yield
                pv = bfv(PS.next())
                for h in range(4):
                    tr(pv[0:C, h * 128:(h + 1) * 128], vT[:, h, c0:c0 + C], identb, sig=(h == 3))
                cp(G["vtok"][0:C, :], pv[0:C, 0:512], e="act")
                pk = bfv(PS.next())
                for h in range(4):
                    tr(pk[0:C, h * 128:(h + 1) * 128], kh[:, h, c0:c0 + C], identb, sig=(h == 3))
                pk3 = Tl(pk.ap[0:C, 0:512].rearrange("p (a b) -> p a b", a=4), pk.b)
                tt(v3(G["kg"], C, 4, 128), pk3, bc(egc[0:C, :], 128), ALU.mult)
                dl = Tl(Dm3.ap[:, :, C - 1], Dm.b)
                tt(v3(G["kdec"], C, 4, 128), pk3, bc(dl, 128), ALU.mult)
                yield
                pxk = PS.next()
                for h in range(4):
                    mm(pxk[:, h * C:(h + 1) * C], G["kg"][0:C, h * 128:(h + 1) * 128], G["R"][0:C, h * C:(h + 1) * C],
                       True, True, sig=(h == 3))
                act(G["nXk"][:, 0:HC], pxk[:, 0:HC], AF.Copy, scale=-1.0)
                yield
                pvn = PS.next()
                for h in range(4):
                    hv = slice(h * 128, (h + 1) * 128)
                    mm(pvn[0:C, hv], G["R"][0:C, h * C:(h + 1) * C], G["vtok"][0:C, hv], True, False, sig=False)
                    mm(pvn[0:C, hv], G["nXk"][:, h * C:(h + 1) * C], Sbf[:, hv], False, True, sig=(h == 3))
                tt(v3(G["vnew"], C, 4, 128), v3(pvn, C, 4, 128), bc(bcol, 128), ALU.mult)
                yield
                po1 = PS.next()
                po2 = PS.next()
                for h in range(4):
                    hv = slice(h * 128, (h + 1) * 128)
                    mm(po1[0:C, hv], qh[:, h, c0:c0 + C], Sbf[:, hv], True, True, sig=False)
                for h in range(4):
                    hv = slice(h * 128, (h + 1) * 128)
                    mm(po2[0:C, hv], G["qkT"][0:C, h * C:(h + 1) * C], G["vnew"][0:C, hv], True, True, sig=(h == 3))
                og3 = v3(og, C, 4, 128)
                tt(og3, v3(po1, C, 4, 128), bc(egc[0:C, :], 128), ALU.mult)
                tt(og[0:C, :], og[0:C, :], po2[0:C, :], ALU.add)
                yield
                pds = PS.next()
                for h in range(4):
                    hv = slice(h * 128, (h + 1) * 128)
                    mm(pds[:, hv], G["kdec"][0:C, hv], G["vnew"][0:C, hv], True, True, sig=(h == 3))
                tt(v3(Sst, 128, 4, 128), v3(Sst, 128, 4, 128), bc(elast, 128), ALU.mult)
                tt(Sst, Sst, pds, ALU.add)
                cp(Sbf, Sst, e="act")
                yield
                sqo = R32.next()
                tt(sqo[0:C, :], og[0:C, :], og[0:C, :], ALU.mult)
                S.op("dve", lambda: nc.vector.reduce_sum(out=ssn.ap[0:C, :], in_=v3(sqo, C, 4, 128).ap, axis=AX.X),
                     reads=[sqo.b], writes=[ssn.b])
                act(ssn[0:C, :], ssn[0:C, :], AF.Sqrt, bias=epsc[0:C, :], scale=1.0 / 128)
                recip(ssn[0:C, :], ssn[0:C, :])
                tt(v3(G["on"], C, 4, 128), og3, bc(ssn[0:C, :], 128), ALU.mult)
                yield
                pt = bfv(PS.next())
                for h in range(4):
                    tr(pt[:, h * C:(h + 1) * C], G["on"][0:C, h * 128:(h + 1) * 128], identb[0:C, 0:C], sig=(h == 3))
                pt3 = Tl(pt.ap[:, 0:HC].rearrange("p (a b) -> p a b", a=4), pt.b)
                stt(mix[:, 4:8, c0:c0 + C], pt3, cvec[:, 112:113], sz[:, :, c0:c0 + C], ALU.mult, ALU.mult)
                yield


        def gdn128_gen():
            SC = 128
            c_same = cst[:, 512:640]
            vnh = [G["vnew"], vnew2]
            memset(G["vnew"][64:128, :], 0.0)
            memset(vnew2[0:64, :], 0.0)
            for s0 in range(0, T, SC):
                cs = slice(s0, s0 + SC)
                p1 = PS.next()
                mm(p1[:, 0:4], betaT[0:4, cs], cst[0:4, 0:4], True, True, sig=False)
                mm(p1[:, 4:8], gT[0:4, cs], cst[0:4, 0:4], True, True)
                cp(bg[:, :], p1[:, 0:8])
                yield
                bcol = bg[:, 0:4]
                gcol = bg[:, 4:8]
                p2 = PS.next()
                mm(p2[:, 0:4], c_incl, gcol, True, True, sig=False)
                ts(gab[:, 0:4], gcol, c_same[:, 0:1], None, ALU.mult)
                ts(gab[:, 4:8], gcol, c_same[:, 64:65], None, ALU.mult)
                mm(p2[:, 8:16], onesf, gab, True, True)
                cp(gcc[:, :], p2[:, 0:4])
                act(egc[:, :], p2[:, 0:4], AF.Exp)
                act(elast8, p2[:, 8:16], AF.Exp)
                yield
                gtri = R32.next()
                tt(v3(gtri, 128, 4, 128), incl4, bc(gcol, 128), ALU.mult)
                p3 = PS.next()
                mm(p3, onesf, gtri, True, True)
                Dm3 = v3(Dm, 128, 4, 128)
                tt(Dm3, v3(p3, 128, 4, 128), bc(gcc, 128), ALU.subtract)
                ts(Dm, Dm, 0.0, None, ALU.min)
                act(Dm, Dm, AF.Exp)
                tt(Dm3, Dm3, incl4, ALU.mult)
                yield
                p4 = PS.next()
                p5 = PS.next()
                for h in range(4):
                    mm(p4[:, h * 128:(h + 1) * 128], kh[:, h, cs], kh[:, h, cs], True, True, sig=False)
                for h in range(4):
                    mm(p5[:, h * 128:(h + 1) * 128], kh[:, h, cs], qh[:, h, cs], True, True, sig=(h == 3))
                tt(G["qkT"], p5, Dm, ALU.mult)
                dms = R32.next()
                dms3 = v3(dms, 128, 4, 128)
                tt(dms3, Dm3, strict4, ALU.mult)
                tt(dms3, dms3, bc(bcol, 128), ALU.mult)
                stt(G["Xa"], p4, -1.0, dms, ALU.mult, ALU.mult)
                yield
                pb = bfv(PS.next())
                for h in range(4):
                    hs = slice(h * 128, (h + 1) * 128)
                    tr(pb[:, hs], G["Xa"][:, hs], identb, sig=(h == 3))
                cp(G["Ya"], pb[:, 0:512], e="act")
                tt(v3(G["R"], 128, 4, 128), v3(G["Xa"], 128, 4, 128), ident4, ALU.add)
                yield
                X, Y, Xn, Yn = G["Xa"], G["Ya"], G["Xb"], G["Yb"]
                for sN in range(1, K + 2):
                    need_x = sN < K
                    need_y = sN <= K
                    need_p = sN >= 2
                    if need_x:
                        px = PS.next()
                        for h in range(4):
                            hs = slice(h * 128, (h + 1) * 128)
                            mm(px[:, hs], Y[:, hs], X[:, hs], True, True, sig=(h == 3))
                    if need_y:
                        py = PS.next()
                        for h in range(4):
                            hs = slice(h * 128, (h + 1) * 128)
                            mm(py[:, hs], X[:, hs], Y[:, hs], True, True, sig=(h == 3))
                    if need_p:
                        pr = PS.next()
                        for h in range(4):
                            hs = slice(h * 128, (h + 1) * 128)
                            mm(pr[:, hs], Y[:, hs], G["R"][:, hs], True, True, sig=(h == 3))
                    if need_x:
                        cp(Xn, px, e="act")
                    if need_y:
                        cp(Yn, py)
                    if need_p:
                        tt(G["R"], G["R"], pr, ALU.add)
                    X, Y, Xn, Yn = Xn, Yn, X, Y
                    yield
                pv = bfv(PS.next())
                for h in range(4):
                    tr(pv[:, h * 128:(h + 1) * 128], vT[:, h, cs], identb, sig=(h == 3))
                cp(G["vtok"], pv[:, 0:512], e="act")
                pk = bfv(PS.next())
                for h in range(4):
                    tr(pk[:, h * 128:(h + 1) * 128], kh[:, h, cs], identb, sig=(h == 3))
                pk3 = Tl(pk.ap[:, 0:512].rearrange("p (a b) -> p a b", a=4), pk.b)
                tt(v3(G["kg"], 128, 4, 128), pk3, bc(egc, 128), ALU.mult)
                for (r0, lastc) in ((0, 63), (64, 127)):
                    dl = Tl(Dm3.ap[r0:r0 + 64, :, lastc], Dm.b)
                    tt(Tl(v3(G["kdec"], 128, 4, 128).ap[r0:r0 + 64], G["kdec"].b),
                       Tl(pk3.ap[r0:r0 + 64], pk.b), bc(dl, 128), ALU.mult)
                yield
                pxk = PS.next()
                pp = PS.next()
                for h in range(4):
                    hs = slice(h * 128, (h + 1) * 128)
                    mm(pxk[:, hs], G["kg"][:, hs], G["R"][:, hs], True, True, sig=False)
                for h in range(4):
                    hs = slice(h * 128, (h + 1) * 128)
                    mm(pp[:, hs], G["R"][:, hs], G["vtok"][:, hs], True, True, sig=(h == 3))
                act(G["nXk"], pxk, AF.Copy, scale=-1.0)
                cp(G["RT"], pp)
                yield
                og3 = v3(og, 128, 4, 128)
                for half, r0 in enumerate((0, 64)):
                    rs_ = slice(r0, r0 + 64)
                    pvn = PS.next()
                    po1 = PS.next()
                    for h in range(4):
                        hs = slice(h * 128, (h + 1) * 128)
                        mm(pvn[:, hs], G["nXk"][:, hs], Sbf[:, hs], True, True, sig=False)
                    for h in range(4):
                        hs = slice(h * 128, (h + 1) * 128)
                        mm(po1[:, hs], qh[:, h, cs], Sbf[:, hs], True, True, sig=(h == 3))
                    tmpv = R32.next()
                    tt(tmpv[rs_, :], pvn[rs_, :], G["RT"][rs_, :], ALU.add)
                    vn_ = vnh[half]
                    tt(Tl(v3(vn_, 128, 4, 128).ap[rs_], vn_.b), Tl(v3(tmpv, 128, 4, 128).ap[rs_], tmpv.b),
                       bc(bg[rs_, 0:4], 128), ALU.mult)
                    tt(Tl(og3.ap[rs_], og.b), Tl(v3(po1, 128, 4, 128).ap[rs_], po1.b), bc(egc[rs_, :], 128), ALU.mult)
                    pds = PS.next()
                    for h in range(4):
                        hs = slice(h * 128, (h + 1) * 128)
                        mm(pds[:, hs], G["kdec"][:, hs], vn_[:, hs], True, True, sig=(h == 3))
                    tt(v3(Sst, 128, 4, 128), v3(Sst, 128, 4, 128), bc(elast8[:, half * 4:half * 4 + 4], 128), ALU.mult)
                    tt(Sst, Sst, pds, ALU.add)
                    cp(Sbf, Sst, e="act")
                    yield
                po2 = PS.next()
                for h in range(4):
                    hs = slice(h * 128, (h + 1) * 128)
                    mm(po2[:, hs], G["qkT"][:, hs], vnh[0][:, hs], True, False, sig=False)
                    mm(po2[:, hs], G["qkT"][:, hs], vnh[1][:, hs], False, True, sig=(h == 3))
                tt(og, og, po2, ALU.add)
                sqo = R32.next()
                tt(sqo, og, og, ALU.mult)
                S.op("dve", lambda: nc.vector.reduce_sum(out=ssn.ap, in_=v3(sqo, 128, 4, 128).ap, axis=AX.X),
                     reads=[sqo.b], writes=[ssn.b])
                act(ssn, ssn, AF.Sqrt, bias=epsc, scale=1.0 / 128)
                recip(ssn, ssn)
                tt(v3(G["on"], 128, 4, 128), og3, bc(ssn, 128), ALU.mult)
                yield
                pt = bfv(PS.next())
                for h in range(4):
                    hs = slice(h * 128, (h + 1) * 128)
                    tr(pt[:, hs], G["on"][:, hs], identb, sig=(h == 3))
                pt3 = Tl(pt.ap[:, 0:512].rearrange("p (a b) -> p a b", a=4), pt.b)
                stt(mix[:, 4:8, cs], pt3, cvec[:, 112:113], sz[:, :, cs], ALU.mult, ALU.mult)
                yield

        def mla_gen():
            uq = [load_w(w_uq_d[l, 0], ("uq", l, 0)), load_w(w_uq_d[l, 1], ("uq", l, 1))]
            ukv = load_w(w_ukv_d[l], ("ukv", l))
            own = []
            if kind == "p":
                for j in range(T // 128):
                    own.append((key0 // 128 + j, 128, j * 128, True))
            else:
                own.append((PAST // 128, TS, 0, False))
            def qprep(h):
                wq = uq[h // 4].ap.rearrange("p (k a n) -> p k a n", k=4, a=4)
                wqb = uq[h // 4].b
                hh = h % 4
                pn = PS.next()
                for kt in range(4):
                    mm(pn[:, 0:T], Tl(wq[:, kt, hh, 0:128], wqb), cqn[:, kt, 0:T], kt == 0, kt == 3)
                cp(qn[:, 0:T], pn[:, 0:T], e="act")
                yield
                pr1 = PS.next()
                for kt in range(4):
                    mm(pr1[0:64, 0:T], Tl(wq[:, kt, hh, 128:192], wqb), cqn[:, kt, 0:T], kt == 0, kt == 3)
                pr2 = PS.next()
                for kt in range(4):
                    mm(pr2[0:64, 0:T], Tl(wq[:, kt, hh, 192:256], wqb), cqn[:, kt, 0:T], kt == 0, kt == 3)
                t1 = R32.next()
                t2 = R32.next()
                tt(t1[0:64, 0:T], pr1[0:64, 0:T], cosT[:, 0:T], ALU.mult)
                tt(t2[0:64, 0:T], pr2[0:64, 0:T], sinT[:, 0:T], ALU.mult)
                tt(qr[:, 0:T], t1[0:64, 0:T], t2[0:64, 0:T], ALU.add)
                yield
                for cc in range(2):
                    pl = PS.next()
                    mm(pl[:, 0:T], ukv[:, h * 256 + cc * 128:h * 256 + (cc + 1) * 128], qn[:, 0:T], True, True)
                    cp(qlat[:, cc, 0:T], pl[:, 0:T], e=("act" if cc == 0 else "dve"))
                    yield

            yield from qprep(0)
            for h in range(8):
                steps = [(kt, 128, 0, False) for kt in range(hist_kt)] + own
                sts = {}

                def emit_score(sj):
                    ktg_, nk_, q0_, _ = steps[sj]
                    ks = slice(ktg_ * 128, ktg_ * 128 + nk_)
                    st_ = PS.next()
                    mm(st_[0:nk_, q0_:T], ckvT[:, 0, ks], qlat[:, 0, q0_:T], True, False, sig=False)
                    mm(st_[0:nk_, q0_:T], ckvT[:, 1, ks], qlat[:, 1, q0_:T], False, False, sig=False)
                    mm(st_[0:nk_, q0_:T], krT[:, ks], qr[:, q0_:T], False, True)
                    sts[sj] = st_

                emit_score(0)
                for si, (ktg, nk, q0, diag) in enumerate(steps):
                    if si + 1 < len(steps):
                        emit_score(si + 1)
                    st = sts.pop(si)
                    pT = R16.next()
                    act(pT[0:nk, q0:T], st[0:nk, q0:T], AF.Exp, scale=C_SCALE)
                    if diag:
                        memset(pT[64:128, q0:q0 + 64], 0.0)
                    lhs = [Vc[0:nk, ktg, 0:128], Vc[0:nk, ktg, 128:256], onesb[0:nk, :]]
                    for a in range(3):
                        mm(ACC[a][:, q0:T], lhs[a], pT[0:nk, q0:T], si == 0, False, sig=False)
                        yield
                for a in range(3):
                    mm(ACC[a][:, 0:T], zerosb, qlat[:, 0, 0:T], False, True, sig=(a == 2))
                if h + 1 < 8:
                    yield from qprep(h + 1)
                rinv = R32.next()
                recip(rinv[:, 0:T], ACC[2][:, 0:T])
                for cc in range(2):
                    tt(OT[:, cc, 0:T], ACC[cc][:, 0:T], rinv[:, 0:T], ALU.mult)
                py = PS.next()
                for cc in range(2):
                    o = 2048 + (cc * 8 + h) * 128
                    mm(py[:, 0:T], ukv[:, o:o + 128], OT[:, cc, 0:T], cc == 0, cc == 1)
                cp(mix[:, 8 + h, 0:T], py[:, 0:T], e="act")
                yield


        if kind == "p":
            n_gdn = (T // 128) * (13 + K)
            ggen = gdn128_gen()
        else:
            n_gdn = (T // C) * (14 + K)
            ggen = gdn_gen()
        n_mla = 8 * (5 + 3 * (hist_kt + (T // 128 if kind == "p" else 1)))
        run_interleaved([(ggen, PS_A, 1), (mla_gen(), PS_B, max(1, int(round(n_mla / n_gdn))))])
        fence(G2_bufs, G1_bufs)

        def evac_res(tag, m, ps):
            tt(xT[:, m, 0:T], xT[:, m, 0:T], ps[:, 0:T], ALU.add)

        for g in range(8):
            slot = load_w(w_out_d[l, g], ("out", l, g))
            proj(slot, 16, [(0, 128, "o", 2 * g), (128, 128, "o", 2 * g + 1)], lambda k: mix[:, k, 0:T], T, evac_res)

        rmsnorm_x(16, T, actb)

        def evac_up(tag, f, ps):
            r = R32.next()
            act(r[:, 0:T], ps[:, 0:T], AF.Relu)
            tt(hid[:, f, 0:T], r[:, 0:T], r[:, 0:T], ALU.mult)

        fence(M_bufs, [hid])
        for fc in range(4):
            for g in range(8):
                slot = load_w(w_up_d[l, fc * 8 + g], ("up", l, fc * 8 + g))
                proj(slot, 16, [(0, 128, "f", 2 * g), (128, 128, "f", 2 * g + 1)], lambda k: actb[:, k, 0:T], T, evac_up)
            for mp in range(8):
                slot = load_w(w_down_d[l, fc, mp], ("down", l, fc, mp))
                w4 = slot.ap.rearrange("p (a k n) -> p a k n", a=2, k=16)
                for mi in range(2):
                    ps = PS.next()
                    for k in range(16):
                        mm(ps[:, 0:T], Tl(w4[:, mi, k, :], slot.b), hid[:, k, 0:T], k == 0, k == 15)
                    evac_res("d", mp * 2 + mi, ps)
        fence([hid], M_bufs)

        if l == 0:
            dcols = slice(tok0, tok0 + T) if kind == "p" else slice(SEQ, SEQ + TS)
            dma("pool", xres_d.rearrange("(c p) t -> p c t", p=128)[:, :, dcols], xT[:, :, 0:T], writes=[bx])
        else:
            rmsnorm_x(94, T, xT)
            ydst = yT_d if kind == "p" else ysT_d
            dma("pool", ydst.rearrange("(c p) t -> p c t", p=128)[:, :, ocols], xT[:, :, 0:T])

    for l in range(2):
        if l == 1:
            convert_w(weight_keys(1))
        dma("pool", cvec, cvec_d[l])
        dma("pool", wsf, wsT_d[l].rearrange("p (g i) -> p g i", g=4))
        dma("pool", bsf, bs_d[l])
        tt(wsm, wsf, Tl(c_gm.ap.unsqueeze(1).to_broadcast([128, 4, 128]), c_gm.b), ALU.mult)
        cp(bsb, bsf)
        act(nexpa, cvec[0:4, 110:111], AF.Exp)
        ts(nexpa, nexpa, -1.0, None, ALU.mult)
        memset(Sst, 0.0)
        memset(Sbf, 0.0)
        memset(ctail_p, 0.0)
        for ti in range(NT_RUN):
            process_tile(l, "p", ti)
        dma("pool", gs_o[l], Sst)
        dma("pool", conv_o[l], ctail_p)
        dma("pool", ckvT[:, :, 0:PAST], ckvpT_d[l].rearrange("(c p) k -> p c k", p=128))
        dma("pool", Vc[:, 0:16, :], ckvp_d[l].rearrange("(kt p) c -> p kt c", p=128))
        dma("pool", krT[:, 0:PAST], krpT_d[l])
        dma("pool", Sst, s0_d[l])
        cp(Sbf, Sst, e="act")
        dma("pool", ctail_s, convp_d[l])
        process_tile(l, "s", 0)
        dma("pool", gss_o[l], Sst)
        dma("pool", convs_o[l], ctail_s)
    S.finish()
    print("[kernel] instructions:", S.n_ins, "sbuf bytes left:", nc.sbuf_bytes_remaining, flush=True)
    return nc, S


OFF_AU, OFF_AV, OFF_BQKV, OFF_BZ, OFF_BBETA, OFF_BALPHA, OFF_CQ, OFF_CKV, OFF_CKR, IN_COLS = (
    0, 512, 1024, 2560, 3072, 3076, 3080, 3592, 3848, 3912)

_PROG = None


def _tile_w(w, nk):
    n = w.shape[1]
    return np.ascontiguousarray(w.reshape(nk, 128, n).transpose(1, 0, 2).reshape(128, nk * n))


def _prep_weights(inp):
    f = np.float32
    sw = np.concatenate([np.arange(32, 64), np.arange(0, 32)])
    w_in = np.asarray(inp["w_in"], f)
    cols = np.concatenate([
        np.arange(0, 3072),
        np.arange(OFF_CQ, OFF_CKV), np.arange(OFF_CKV, OFF_CKR),
        np.arange(OFF_CKR, IN_COLS), OFF_CKR + sw,
        np.arange(OFF_BBETA, OFF_BALPHA), np.arange(OFF_BALPHA, OFF_CQ)])
    w_in_t = np.zeros((2, 16, 128, 4096), f)
    for l in range(2):
        wp = np.zeros((D, 4096), f)
        wp[:, :cols.size] = w_in[l][:, cols]
        for g in range(16):
            w_in_t[l, g] = _tile_w(wp[:, g * 256:(g + 1) * 256], 16)
    w_out = np.asarray(inp["w_out"], f)
    w_out_t = np.stack([np.stack([_tile_w(w_out[l][:, g * 256:(g + 1) * 256], 16) for g in range(8)]) for l in range(2)])
    w_up = np.asarray(inp["w_up"], f)
    w_up_t = np.stack([np.stack([_tile_w(w_up[l][:, g * 256:(g + 1) * 256], 16) for g in range(32)]) for l in range(2)])
    w_down = np.asarray(inp["w_down"], f)
    w_down_t = np.zeros((2, 4, 8, 128, 4096), f)
    for l in range(2):
        for fc in range(4):
            for mp in range(8):
                blk = w_down[l][fc * 2048:(fc + 1) * 2048, mp * 256:(mp + 1) * 256]
                t = blk.reshape(16, 128, 2, 128).transpose(1, 2, 0, 3)
                w_down_t[l, fc, mp] = t.reshape(128, 4096)
    w_uq = np.asarray(inp["c_w_uq"], f)
    w_uq_t = np.zeros((2, 2, 128, 4096), f)
    for l in range(2):
        per_head = np.concatenate([w_uq[l][:, :, 0:128], w_uq[l][:, :, 128:192], w_uq[l][:, :, 128 + sw]], axis=2)
        for hg in range(2):
            blk = per_head[:, hg * 4:(hg + 1) * 4, :].reshape(512, 1024)
            w_uq_t[l, hg] = _tile_w(blk, 4)
    w_uk = np.asarray(inp["c_w_uk"], f)
    w_uv = np.asarray(inp["c_w_uv"], f)
    w_ukv_t = np.zeros((2, 128, 4096), f)
    for l in range(2):
        w_ukv_t[l, :, 0:2048] = w_uk[l].transpose(2, 1, 0).reshape(128, 2048)
        w_ukv_t[l, :, 2048:4096] = w_uv[l].reshape(2, 128, 8, 128).transpose(1, 0, 2, 3).reshape(128, 2048)
    cvec = np.zeros((2, 128, 128), f)

    def pc(v):
        return np.asarray(v, f).reshape(-1, 128).T

    for l in range(2):
        cvec[l, :, 0:16] = pc(inp["attn_norm_w"][l])
        cvec[l, :, 16:32] = pc(inp["mlp_norm_w"][l])
        cvec[l, :, 32:36] = pc(inp["a_ln_w"][l])
        cvec[l, :, 36:40] = pc(inp["a_ln_b"][l])
        cw = np.asarray(inp["b_conv_w"][l], f)
        cvec[l, :, 40:88] = cw.reshape(4, 12, 128).transpose(2, 1, 0).reshape(128, 48)
        cvec[l, :, 88:92] = pc(inp["c_q_norm_w"][l])
        cvec[l, :, 92:94] = pc(inp["c_kv_norm_w"][l])
        cvec[l, :, 94:110] = pc(inp["final_norm_w"])
        cvec[l, 0:4, 110] = np.asarray(inp["b_a_log"][l], f)
        cvec[l, 0:4, 111] = np.asarray(inp["b_dt_bias"][l], f)
        cvec[l, :, 112] = np.asarray(inp["b_norm_w"][l], f)
    a_w_s = np.asarray(inp["a_w_s"], f)
    wsT = np.ascontiguousarray(a_w_s.transpose(0, 3, 1, 2).reshape(2, 128, 512))
    bs = np.ascontiguousarray(np.asarray(inp["a_b_s"], f).reshape(2, 1, 512))
    pos = np.concatenate([np.arange(SEQ), PAST + np.arange(TS)]).astype(f)
    inv = (np.float32(10000.0) ** (-np.arange(32, dtype=f) / np.float32(32))).astype(f)
    ang = (pos[None, :] * inv[:, None]).astype(f)
    cos, sin = np.cos(ang).astype(f), np.sin(ang).astype(f)
    cos2 = np.ascontiguousarray(np.concatenate([cos, cos], 0))
    sin2 = np.ascontiguousarray(np.concatenate([-sin, sin], 0))
    idx = np.arange(128)
    same = (idx[:, None] // 64) == (idx[None, :] // 64)
    cst = np.concatenate([
        np.eye(128, dtype=f),
        ((idx[:, None] <= idx[None, :]) & same).astype(f),
        ((idx[:, None] < idx[None, :]) & same).astype(f),
        ((idx[None, :] // 64) >= (idx[:, None] // 64)).astype(f),
        same.astype(f)], axis=1)
    return dict(w_in_t=w_in_t, w_out_t=w_out_t, w_up_t=w_up_t, w_down_t=w_down_t, w_uq_t=w_uq_t,
                w_ukv_t=w_ukv_t, cvec=cvec, wsT=wsT, bs=bs, cos2=cos2, sin2=sin2,
                cst=np.ascontiguousarray(cst))


def _prep_inputs(inp):
    f = np.float32
    shared = _prep_weights(inp)
    xp = np.asarray(inp["x_prompt"], f)
    xs = np.asarray(inp["x_sample"], f)
    ckv = np.asarray(inp["cache_mla_ckv"], f)
    kr = np.asarray(inp["cache_mla_krope"], f)
    sg = np.asarray(inp["state_gdn"], f)
    sc = np.asarray(inp["state_gdn_conv"], f)
    maps = []
    for c in range(8):
        m = dict(shared)
        m["xT"] = np.ascontiguousarray(xp[c].T) if c < 4 else np.zeros((D, SEQ), f)
        m["xsT"] = np.ascontiguousarray(xs[c].T)
        m["ckvpT"] = np.ascontiguousarray(ckv[:, c].transpose(0, 2, 1))
        m["ckvp"] = np.ascontiguousarray(ckv[:, c])
        m["krpT"] = np.ascontiguousarray(kr[:, c].transpose(0, 2, 1))
        m["s0"] = np.ascontiguousarray(sg[:, c].transpose(0, 2, 1, 3).reshape(2, 128, 512))
        m["convp"] = np.ascontiguousarray(sc[:, c].reshape(2, 3, 12, 128).transpose(0, 3, 2, 1).reshape(2, 128, 36))
        maps.append(m)
    return maps


def _conv_back(a):
    return a.reshape(2, 128, 12, 3).transpose(0, 3, 2, 1).reshape(2, 3, 1536)


def _state_back(a):
    return a.reshape(2, 128, 4, 128).transpose(0, 2, 1, 3)


def kernel(**inputs):
    global _PROG
    if _PROG is None:
        _PROG = build_program()[0]
    nc = _PROG
    in_maps = _prep_inputs(inputs)
    res = run_bass_kernel_spmd(nc, in_maps, core_ids=list(range(8)))
    r = res.results
    f = np.float32
    y_prompt = np.stack([r[c]["yT"].T for c in range(4)]).astype(f)
    y_sample = np.stack([r[c]["ysT"].T for c in range(8)]).astype(f)
    ckv_p = np.stack([r[c]["ckv_o"].transpose(0, 2, 1) for c in range(4)], axis=1).astype(f)
    kr_p = np.stack([r[c]["kr_o"].transpose(0, 2, 1) for c in range(4)], axis=1).astype(f)
    gs_p = np.stack([_state_back(r[c]["gs_o"]) for c in range(4)], axis=1).astype(f)
    conv_p = np.stack([_conv_back(r[c]["conv_o"]) for c in range(4)], axis=1).astype(f)
    ckv_s = np.stack([r[c]["ckvs_o"].transpose(0, 2, 1) for c in range(8)], axis=1).astype(f)
    kr_s = np.stack([r[c]["krs_o"].transpose(0, 2, 1) for c in range(8)], axis=1).astype(f)
    gs_s = np.stack([_state_back(r[c]["gss_o"]) for c in range(8)], axis=1).astype(f)
    conv_s = np.stack([_conv_back(r[c]["convs_o"]) for c in range(8)], axis=1).astype(f)
    vn_s = np.stack([r[c]["vns_o"].transpose(0, 2, 1) for c in range(8)], axis=1).astype(f)
    return (np.ascontiguousarray(y_prompt), np.ascontiguousarray(y_sample),
            np.ascontiguousarray(ckv_p), np.ascontiguousarray(kr_p), np.ascontiguousarray(gs_p),
            np.ascontiguousarray(conv_p), np.ascontiguousarray(ckv_s), np.ascontiguousarray(kr_s),
            np.ascontiguousarray(gs_s), np.ascontiguousarray(conv_s), np.ascontiguousarray(vn_s))
```
